# Optimizing a Trainium2 kernel written in Bass

```python
import jax, jax.numpy as jnp
from jax import lax
import numpy as np

D_MODEL = 1024
BATCH = 16
SEQ = 4096
DEPTH = 2
DEC_BATCH = 16
DEC_SEQ = 64
PAST_LEN = 1024

CHUNK = 64
N_A_LAYERS = DEPTH // 2
N_B_LAYERS = DEPTH - N_A_LAYERS
POOL_WINDOWS = (2, 4, 8, 16)
N_POOL_GROUPS = len(POOL_WINDOWS)
POOL_GROUP_WIDTH = D_MODEL // N_POOL_GROUPS
POOL_HIST = max(POOL_WINDOWS) - 1
HEAD_DIM = 64
N_HEADS = D_MODEL // HEAD_DIM
ATTN_WIDTH = N_HEADS * HEAD_DIM
D_FF = -(-8 * D_MODEL // (3 * 256)) * 256
Q_BLOCK = 128
RMS_EPS = 1e-6
NEG_INF = -1e30
FORGET_BIAS_INIT = 2.0

kernel_name = "yoco_pool_fox_streaming_step"


def rms_norm(x, g):
    xf = x.astype(jnp.float32)
    y = xf * lax.rsqrt(jnp.mean(xf * xf, axis=-1, keepdims=True) + RMS_EPS)
    return (y * g.astype(jnp.float32)).astype(x.dtype)


def swiglu_ffn(h, w_gate, w_up, w_down):
    return (jax.nn.silu(h @ w_gate) * (h @ w_up)) @ w_down


def pool_mixer(h, prev, pos0, w_groups, scale):
    S = h.shape[1]
    ext = jnp.concatenate([prev.astype(h.dtype), h], axis=1)
    cs = jnp.cumsum(ext.astype(jnp.float32), axis=1)
    cs = jnp.pad(cs, ((0, 0), (1, 0), (0, 0)))
    end = cs[:, POOL_HIST + 1:POOL_HIST + 1 + S]
    pos = pos0 + jnp.arange(S, dtype=jnp.int32)
    hf = h.astype(jnp.float32)
    outs = []
    for g, w in enumerate(POOL_WINDOWS):
        sl = slice(g * POOL_GROUP_WIDTH, (g + 1) * POOL_GROUP_WIDTH)
        start = cs[:, POOL_HIST + 1 - w:POOL_HIST + 1 - w + S, sl]
        cnt = jnp.minimum(pos + 1, w).astype(jnp.float32)[None, :, None]
        pooled = (end[..., sl] - start) / cnt - hf[..., sl]
        outs.append(jnp.einsum('bsc,cd->bsd', pooled, w_groups[g].astype(jnp.float32)))
    out = jnp.concatenate(outs, axis=-1) * scale.astype(jnp.float32)
    return out.astype(h.dtype), ext[:, -POOL_HIST:]


def shared_kv(x, g_kv, w_k, w_v, w_f, b_f, g_knorm):
    B, S, _ = x.shape
    h = rms_norm(x, g_kv)
    k = rms_norm((h @ w_k).reshape(B, S, N_HEADS, HEAD_DIM), g_knorm)
    v = (h @ w_v).reshape(B, S, N_HEADS, HEAD_DIM)
    logf = jax.nn.log_sigmoid((h @ w_f + b_f).astype(jnp.float32))
    return k, v, logf


def fox_attend(q, dq, k, dk, v, q_pos, k_pos):
    s = jnp.einsum('bqhd,bkhd->bhqk', q.astype(jnp.float32), k.astype(jnp.float32)) * (HEAD_DIM ** -0.5)
    s = s + (jnp.transpose(dq, (0, 2, 1))[..., :, None] - jnp.transpose(dk, (0, 2, 1))[..., None, :])
    mask = k_pos[None, :] <= q_pos[:, None]
    p = jax.nn.softmax(jnp.where(mask, s, NEG_INF), axis=-1)
    return jnp.einsum('bhqk,bkhd->bqhd', p, v.astype(jnp.float32))


def fox_prompt(q, k, v, logf):
    B, S, H, Dh = q.shape
    d = jnp.cumsum(logf.astype(jnp.float32), axis=1)
    nblk = S // Q_BLOCK
    qb = q.reshape(B, nblk, Q_BLOCK, H, Dh).transpose(1, 0, 2, 3, 4)
    dqb = d.reshape(B, nblk, Q_BLOCK, H).transpose(1, 0, 2, 3)
    starts = jnp.arange(nblk, dtype=jnp.int32) * Q_BLOCK
    k_pos = jnp.arange(S, dtype=jnp.int32)

    def one_block(args):
        qi, dqi, st = args
        return fox_attend(qi, dqi, k, d, v, st + jnp.arange(Q_BLOCK, dtype=jnp.int32), k_pos)

    o = lax.map(one_block, (qb, dqb, starts))
    return o.transpose(1, 0, 2, 3, 4).reshape(B, S, H * Dh)


def fox_sample(q, k_all, v_all, logf_all, past):
    B, T, H, Dh = q.shape
    d = jnp.cumsum(logf_all.astype(jnp.float32), axis=1)
    k_pos = jnp.arange(past + T, dtype=jnp.int32)
    q_pos = past + jnp.arange(T, dtype=jnp.int32)
    o = fox_attend(q, d[:, past:], k_all, d, v_all, q_pos, k_pos)
    return o.reshape(B, T, H * Dh)


def setup_inputs(seed: int = 0) -> dict:
    key = jax.random.key(seed)
    ks = jax.random.split(key, 24)

    def nrm(k, shape, s):
        return jax.random.normal(k, shape, jnp.float32) * s

    return {
        'x_prompt': nrm(ks[0], (BATCH, SEQ, D_MODEL), 1.0),
        'x_sample': nrm(ks[1], (DEC_BATCH, DEC_SEQ, D_MODEL), 1.0),
        'state_pool': nrm(ks[2], (N_A_LAYERS, DEC_BATCH, POOL_HIST, D_MODEL), 1.0),
        'cache_k': nrm(ks[3], (DEC_BATCH, PAST_LEN, N_HEADS, HEAD_DIM), 1.0),
        'cache_v': nrm(ks[4], (DEC_BATCH, PAST_LEN, N_HEADS, HEAD_DIM), 1.0),
        'cache_logf': jax.nn.log_sigmoid(FORGET_BIAS_INIT + nrm(ks[5], (DEC_BATCH, PAST_LEN, N_HEADS), 1.0)),
        'g_mix': 1.0 + nrm(ks[6], (DEPTH, D_MODEL), 0.02),
        'g_ffn': 1.0 + nrm(ks[7], (DEPTH, D_MODEL), 0.02),
        'pool_w': nrm(ks[8], (N_A_LAYERS, N_POOL_GROUPS, POOL_GROUP_WIDTH, POOL_GROUP_WIDTH), POOL_GROUP_WIDTH ** -0.5),
        'pool_scale': 1.0 + nrm(ks[9], (N_A_LAYERS, D_MODEL), 0.02),
        'g_kv': 1.0 + nrm(ks[10], (D_MODEL,), 0.02),
        'w_k': nrm(ks[11], (D_MODEL, ATTN_WIDTH), D_MODEL ** -0.5),
        'w_v': nrm(ks[12], (D_MODEL, ATTN_WIDTH), D_MODEL ** -0.5),
        'w_f': nrm(ks[13], (D_MODEL, N_HEADS), D_MODEL ** -0.5),
        'b_f': FORGET_BIAS_INIT + nrm(ks[14], (N_HEADS,), 0.1),
        'g_knorm': 1.0 + nrm(ks[15], (HEAD_DIM,), 0.02),
        'w_q': nrm(ks[16], (N_B_LAYERS, D_MODEL, ATTN_WIDTH), D_MODEL ** -0.5),
        'g_qnorm': 1.0 + nrm(ks[17], (N_B_LAYERS, HEAD_DIM), 0.02),
        'w_o': nrm(ks[18], (N_B_LAYERS, ATTN_WIDTH, D_MODEL), ATTN_WIDTH ** -0.5),
        'w_gate': nrm(ks[19], (DEPTH, D_MODEL, D_FF), D_MODEL ** -0.5),
        'w_up': nrm(ks[20], (DEPTH, D_MODEL, D_FF), D_MODEL ** -0.5),
        'w_down': nrm(ks[21], (DEPTH, D_FF, D_MODEL), D_FF ** -0.5),
    }


def reference(x_prompt, x_sample, state_pool, cache_k, cache_v, cache_logf,
              g_mix, g_ffn, pool_w, pool_scale,
              g_kv, w_k, w_v, w_f, b_f, g_knorm,
              w_q, g_qnorm, w_o,
              w_gate, w_up, w_down):
    past = cache_k.shape[1]
    B, S, _ = x_prompt.shape
    Bs, Ss, _ = x_sample.shape
    xp, xs = x_prompt, x_sample
    pool_new_p, pool_new_s = [], []
    for l in range(DEPTH):
        if l < N_A_LAYERS:
            hp = rms_norm(xp, g_mix[l])
            hs = rms_norm(xs, g_mix[l])
            yp, stp = pool_mixer(hp, jnp.zeros((B, POOL_HIST, D_MODEL), hp.dtype), 0, pool_w[l], pool_scale[l])
            ys, sts = pool_mixer(hs, state_pool[l], past, pool_w[l], pool_scale[l])
            pool_new_p.append(stp)
            pool_new_s.append(sts)
            xp = xp + yp
            xs = xs + ys
        else:
            j = l - N_A_LAYERS
            if j == 0:
                k_p, v_p, logf_p = shared_kv(xp, g_kv, w_k, w_v, w_f, b_f, g_knorm)
                k_s, v_s, logf_s = shared_kv(xs, g_kv, w_k, w_v, w_f, b_f, g_knorm)
                k_all = jnp.concatenate([cache_k.astype(k_s.dtype), k_s], axis=1)
                v_all = jnp.concatenate([cache_v.astype(v_s.dtype), v_s], axis=1)
                logf_all = jnp.concatenate([cache_logf.astype(jnp.float32), logf_s], axis=1)
            hp = rms_norm(xp, g_mix[l])
            hs = rms_norm(xs, g_mix[l])
            qp = rms_norm((hp @ w_q[j]).reshape(B, S, N_HEADS, HEAD_DIM), g_qnorm[j])
            qs = rms_norm((hs @ w_q[j]).reshape(Bs, Ss, N_HEADS, HEAD_DIM), g_qnorm[j])
            op = fox_prompt(qp, k_p, v_p, logf_p).astype(xp.dtype)
            os_ = fox_sample(qs, k_all, v_all, logf_all, past).astype(xs.dtype)
            xp = xp + op @ w_o[j]
            xs = xs + os_ @ w_o[j]
        xp = xp + swiglu_ffn(rms_norm(xp, g_ffn[l]), w_gate[l], w_up[l], w_down[l])
        xs = xs + swiglu_ffn(rms_norm(xs, g_ffn[l]), w_gate[l], w_up[l], w_down[l])
    pool_state_p = jnp.stack(pool_new_p, axis=0)
    pool_state_s = jnp.stack(pool_new_s, axis=0)
    return (xp, xs, pool_state_p, pool_state_s, k_p, v_p, logf_p, k_s, v_s, logf_s)
```

```python
import contextlib
import numpy as np
import ml_dtypes
import concourse.bass as bass
import concourse.mybir as mybir
from concourse.bass_utils import run_bass_kernel_spmd

F32 = mybir.dt.float32
BF16 = mybir.dt.bfloat16
AF = mybir.ActivationFunctionType
ALU = mybir.AluOpType
AX = mybir.AxisListType

D = 1024
DFF = 2816
NF = 22
NH = 16
DH = 64
S_PROMPT = 4096
PAST = 1024
SS = 64
NCORES = 8
R_RING = 5
DEBUG_MEMSET = False
EPS = 1e-6
MASKV = -240000.0
POOL_W = (2, 4, 8, 16)

NBAND = 20


def _band_consts():
    b = np.zeros((NBAND, 128, 128), np.float32)
    s = np.arange(128)[:, None]
    t = np.arange(128)[None, :]
    for g, w in enumerate(POOL_W):
        inwin = (s <= t) & (s > t - w)
        b[g] = inwin / w - (s == t)
        b[4 + g] = ((s - 128) > (t - w)) / w
        cnt = np.minimum(t + 1, w)
        b[8 + g] = inwin / cnt - (s == t)
        same = (s // 64) == (t // 64)
        b[12 + g] = (inwin & same) / w - (s == t)
        hs = np.zeros((128, 128), np.float32)
        for q in range(2):
            for i in range(15):
                for tl in range(64):
                    if i > 15 + tl - w:
                        hs[16 * q + i, 64 * q + tl] = 1.0 / w
        b[16 + g] = hs
    return b


def _consts():
    s = np.arange(128)[:, None]
    t = np.arange(128)[None, :]
    ident = np.eye(128, dtype=np.float32)
    mask = np.where(s > t, MASKV, 0.0).astype(np.float32)
    bands = _band_consts()
    cb = np.concatenate([ident[:, None, :], mask[:, None, :], bands.transpose(1, 0, 2)], axis=1)
    cb = cb.reshape(128, (2 + NBAND) * 128).astype(ml_dtypes.bfloat16)
    U = (s <= t).astype(np.float32)
    ONES = np.ones((128, 128), np.float32)
    same = ((s // 64) == (t // 64)).astype(np.float32)
    cf = np.concatenate([U, ONES, U * same, same], axis=1)
    return cb, cf


class Sched:
    def __init__(self):
        self.q = {e: [] for e in ("pe", "act", "dve", "pool", "sp")}
        self.cnt = {}
        self.waited = {e: {} for e in self.q}
        self.res = {}

    def _deps(self, reads, writes, eng):
        deps = {}

        def add(tok):
            if tok is None:
                return
            sname, v = tok
            if deps.get(sname, 0) < v:
                deps[sname] = v

        for r in reads:
            st = self.res.get(r)
            if st:
                add(st["w"])
        for w in writes:
            st = self.res.get(w)
            if st:
                add(st["w"])
                for sname, v in st["r"].items():
                    add((sname, v))
        if eng == "pe":
            deps.pop("e_pe", None)
        return deps

    def _mark(self, tok, reads, writes):
        sname, v = tok
        for r in reads:
            st = self.res.setdefault(r, {"w": None, "r": {}})
            if st["r"].get(sname, 0) < v:
                st["r"][sname] = v
        for w in writes:
            self.res[w] = {"w": tok, "r": {}}

    def _waits(self, eng, deps):
        for sname, v in deps.items():
            if self.waited[eng].get(sname, 0) >= v:
                continue
            self.waited[eng][sname] = v
            self.q[eng].append(("wait", sname, v))

    def op(self, eng, fn, reads=(), writes=()):
        deps = self._deps(reads, writes, eng)
        self._waits(eng, deps)
        sname = "e_" + eng
        self.cnt[sname] = self.cnt.get(sname, 0) + 1
        tok = (sname, self.cnt[sname])
        self.q[eng].append(("op", fn, sname, 1))
        self._mark(tok, reads, writes)
        return tok

    def dma(self, eng, sname, out, in_, reads=(), writes=(), drain=False):
        deps = self._deps(reads, writes, eng)
        self._waits(eng, deps)
        self.cnt[sname] = self.cnt.get(sname, 0) + 16
        tok = (sname, self.cnt[sname])
        self.q[eng].append(("op", (lambda e, o=out, i=in_: e.dma_start(out=o, in_=i)), sname, 16))
        self._mark(tok, reads, writes)
        if drain:
            self._waits(eng, {sname: tok[1]})
        return tok

    def wait_tok(self, eng, tok):
        self._waits(eng, {tok[0]: tok[1]})


def build_program(S=S_PROMPT, past=PAST, ss_len=SS):
    nc = bass.Bass("TRN2", target_bir_lowering=False)
    sch = Sched()
    assert S % 512 == 0 and past % 512 == 0 and ss_len == 64
    KMAX = max(S, past + 128)

    def din(name, shape, dt=F32):
        return nc.dram_tensor(name, list(shape), dt, kind="ExternalInput").ap()

    def dout(name, shape, dt=F32):
        return nc.dram_tensor(name, list(shape), dt, kind="ExternalOutput").ap()

    def dscr(name, shape, dt=BF16):
        return nc.dram_tensor(name, list(shape), dt, kind="Internal").ap()

    xp = din("xp", [2, S, D]); xs = din("xs", [2, ss_len, D]); spl = din("spl", [2, 15, D])
    ck = din("ck", [2, past, D]); cv = din("cv", [2, past, D]); cl = din("cl", [2, past, NH])
    pool_w = din("pool_w", [D, 256]); w_k = din("w_k", [D, D]); w_v = din("w_v", [D, D])
    w_f = din("w_f", [D, NH]); w_q = din("w_q", [D, D]); w_o = din("w_o", [D, D])
    w_gate = din("w_gate", [2, D, DFF]); w_up = din("w_up", [2, D, DFF]); w_down = din("w_down", [2, DFF, D])
    cbf_d = din("cbf", [128, (2 + NBAND) * 128], BF16); cf_d = din("cf32", [128, 512])
    gbc_d = din("gbc", [128, 2 * D + 64 + 64 + 16 + 32])

    y_p = dout("y_p", [2, S, D]); y_s = dout("y_s", [2, ss_len, D])
    ps_p = dout("ps_p", [2, 15, D]); ps_s = dout("ps_s", [2, 15, D])
    k_p = dout("k_p", [2, S, D]); v_p = dout("v_p", [2, S, D]); lf_p = dout("lf_p", [2, S, NH])
    k_s = dout("k_s", [2, ss_len, D]); v_s = dout("v_s", [2, ss_len, D]); lf_s = dout("lf_s", [2, ss_len, NH])

    wb_pool = dscr("wb_pool", [D, 256]); wb_k = dscr("wb_k", [D, D]); wb_v = dscr("wb_v", [D, D])
    wb_q = dscr("wb_q", [D, D]); wb_o = dscr("wb_o", [D, D])
    wb_gate = dscr("wb_gate", [2, D, DFF]); wb_up = dscr("wb_up", [2, D, DFF]); wb_down = dscr("wb_down", [2, DFF, D])
    KTs = dscr("KTs", [4, NH, 68, KMAX]); Vs = dscr("Vs", [4, 8, KMAX, 192])

    es = contextlib.ExitStack()

    def sb(name, shape, dt):
        return es.enter_context(nc.sbuf_tensor(name, list(shape), dt))

    def ps(name, shape, dt):
        return es.enter_context(nc.psum_tensor(name, list(shape), dt))

    with es:
        x = sb("x", [128, 4, D], F32)
        h = sb("h", [128, 4, D], BF16)
        hprev = sb("hprev", [128, D], BF16)
        hTa = sb("hTa", [128, 8, 512], BF16)
        regA = sb("regA", [128, NF, 512], BF16)
        regB = sb("regB", [128, 2, 512], F32)
        ring = sb("ring", [128, R_RING, 4096], BF16)
        Fp = sb("Fp", [128, 3, D], F32)
        qk_aug = sb("qk_aug", [128, 4, NH, 68], BF16)
        pcA = sb("pcA", [128, 4, NH], F32); pcB = sb("pcB", [128, 4, NH], F32)
        Vaug = sb("Vaug", [128, 4, 8, 192], BF16)
        QT = sb("QT", [128, NH, 512], BF16)
        KT = sb("KT", [128, NH, 512], BF16)
        KTold = sb("KTold", [128, 3, 1024], BF16)
        Vold = sb("Vold", [128, 3, 8, 128], BF16)
        PT = sb("PT", [128, 4, 512], BF16)
        negD = sb("negD", [128, KMAX // 128 + 2, NH], F32)
        negDs = sb("negDs", [128, 2 * (past // 128), NH], F32)
        cbf = sb("cbf_sb", [128, 2 + NBAND, 128], BF16)
        cf = sb("cf_sb", [128, 4, 128], F32)
        gbc = sb("gbc_sb", [128, 2 * D + 64 + 64 + 16 + 32], F32)
        wf = sb("wf_sb", [128, 8, NH], BF16)
        spool = sb("spool", [32, D], BF16)
        lfc = sb("lfc", [128, past // 128, NH], F32)
        ssq = sb("ssq", [128, 4], F32); lnv = sb("lnv", [128, 4], F32); rstd = sb("rstd", [128, 4], F32)
        ssh = sb("ssh", [128, NH], F32); lnh = sb("lnh", [128, NH], F32); rsh = sb("rsh", [128, NH], F32)
        zb = sb("zb", [128, NH], F32); az = sb("az", [128, NH], F32); ez = sb("ez", [128, NH], F32)
        lz = sb("lz", [128, NH], F32); mz = sb("mz", [128, NH], F32)
        lf = sb("lf", [128, 4, NH], F32)
        Dt = sb("Dt", [128, 4, NH], F32)
        Dprev = sb("Dprev", [128, NH], F32)
        DprevS = sb("DprevS", [128, NH], F32)

        P = [ps("P%d" % i, [128, 1024], F32) for i in range(3)]
        T = [ps("T%d" % i, [128, 1024], BF16) for i in range(2)]

        ident = cbf[:, 0, :]
        masktri = cbf[:, 1, :]

        def band(i):
            return cbf[:, 2 + i, :]

        gmix0_bc = gbc[:, 0:D]
        pscale_bc = gbc[:, D:2 * D]
        gq_bc = gbc[:, 2 * D:2 * D + 64]
        gk_bc = gbc[:, 2 * D + 64:2 * D + 128]
        bf_bc = gbc[:, 2 * D + 128:2 * D + 144]
        gcols = gbc[:, 2 * D + 144:2 * D + 176]
        aT = regA
        hTb = regA[:, 0:8, :]
        OT = regA[:, 8:16, :]
        regBf = regB[:, :, :].rearrange("p a b -> p (a b)")

        def A(f, j):
            return ("A", f, j)

        ctok = []
        ctok.append(sch.dma("sp", "d_const", cbf[:, :, :].rearrange("p a b -> p (a b)"), cbf_d[:, :], writes=["c1"]))
        ctok.append(sch.dma("sp", "d_const", cf[:, :, :].rearrange("p a b -> p (a b)"), cf_d[:, :], writes=["c2"]))
        ctok.append(sch.dma("sp", "d_const", gbc[:, :], gbc_d[:, :], writes=["c3"]))
        for e in ("pe", "act", "dve", "pool"):
            sch.wait_tok(e, ctok[-1])

        prep_n = [0]

        def prep(dst, src, key):
            sch.dma("pool", "d_prep%d" % prep_n[0], dst, src, writes=[("wscr", key)], drain=True)
            prep_n[0] += 1

        prep(wb_pool[:, :], pool_w[:, :], "wp")
        GCH = [(0, 1024), (1024, 2048), (2048, DFF)]
        for l in range(2):
            for ci, (c0_, c1_) in enumerate(GCH):
                prep(wb_gate[l][:, c0_:c1_], w_gate[l][:, c0_:c1_], ("g", l, ci))
                prep(wb_up[l][:, c0_:c1_], w_up[l][:, c0_:c1_], ("u", l, ci))
            if l == 0:
                prep(wb_down[l], w_down[l], ("d", 0))
                prep(wb_k[:, :], w_k[:, :], "wk"); prep(wb_v[:, :], w_v[:, :], "wv")
                prep(wb_q[:, :], w_q[:, :], "wq"); prep(wb_o[:, :], w_o[:, :], "wo")
        prep(wb_down[1], w_down[1], ("d", 1))
        sch.dma("pool", "d_wf", wf[:, :, :], w_f.rearrange("(k p) c -> p k c", p=128), writes=["wf"])
        sch.op("dve", lambda e: e.memset(Vaug[:, :, :, :].rearrange("p a b c -> p (a b c)"), 1.0),
               writes=[("VA", j) for j in range(4)])
        sch.op("dve", lambda e: e.memset(spool[:, :], 0.0), writes=["spool"])
        if DEBUG_MEMSET:
            sch.op("dve", lambda e: e.memset(QT[:, :, :].rearrange("p a b -> p (a b)"), 0.0), writes=[])
            sch.op("dve", lambda e: e.memset(KT[:, :, :].rearrange("p a b -> p (a b)"), 0.0), writes=[])

        tile_blocks = ([("wp",)] + [("gu", 0, b) for b in range(11)]
                       + [("dn", 0, dh, fb) for dh in range(2) for fb in range(3)]
                       + [("wk", hf) for hf in range(2)] + [("wv", hf) for hf in range(2)]
                       + [("wq", hf) for hf in range(2)] + [("wo", hf) for hf in range(2)]
                       + [("gu", 1, b) for b in range(11)]
                       + [("dn", 1, dh, fb) for dh in range(2) for fb in range(3)])
        n_tiles = 2 * (S // 512) + 1
        blocks = tile_blocks * n_tiles
        ring_state = {"take": 0, "rel": 0, "issued": 0}

        def ring_issue():
            bi = ring_state["issued"]
            if bi >= len(blocks):
                return
            ring_state["issued"] += 1
            s = bi % R_RING
            b = blocks[bi]
            sem = "d_ring%d" % s
            if b[0] == "wp":
                rd = [("wscr", "wp")]
            elif b[0] == "gu":
                ci_ = min(b[2] // 4, 2)
                rd = [("wscr", ("g", b[1], ci_)), ("wscr", ("u", b[1], ci_))]
            elif b[0] == "dn":
                rd = [("wscr", ("d", b[1]))]
            else:
                rd = [("wscr", b[0])]
            wr = [("ring", s)]
            kp = "(k p) c -> p k c"
            if b[0] == "wp":
                sch.dma("sp", sem, ring[:, s, 0:2048].rearrange("p (k c) -> p k c", c=256),
                        wb_pool.rearrange(kp, p=128), rd, wr)
            elif b[0] == "gu":
                _, l, bb = b
                sch.dma("sp", sem, ring[:, s, 0:2048].rearrange("p (k c) -> p k c", c=256),
                        wb_gate[l][:, bb * 256:(bb + 1) * 256].rearrange(kp, p=128), rd, wr)
                sch.dma("sp", sem, ring[:, s, 2048:4096].rearrange("p (k c) -> p k c", c=256),
                        wb_up[l][:, bb * 256:(bb + 1) * 256].rearrange(kp, p=128), rd, wr)
            elif b[0] == "dn":
                _, l, dh, fb = b
                nf = 8 if fb < 2 else 6
                sch.dma("sp", sem, ring[:, s, 0:nf * 512].rearrange("p (f c) -> p f c", c=512),
                        wb_down[l][fb * 1024:fb * 1024 + nf * 128, dh * 512:(dh + 1) * 512].rearrange("(f p) c -> p f c", p=128),
                        rd, wr)
            else:
                src = {"wk": wb_k, "wv": wb_v, "wq": wb_q, "wo": wb_o}[b[0]]
                hf = b[1]
                sch.dma("sp", sem, ring[:, s, :].rearrange("p (k c) -> p k c", c=512),
                        src[:, hf * 512:(hf + 1) * 512].rearrange(kp, p=128), rd, wr)

        def ring_take(kind):
            bi = ring_state["take"]
            assert blocks[bi][0] == kind, (blocks[bi], kind)
            ring_state["take"] += 1
            return bi % R_RING

        def ring_release(n=1):
            for _ in range(n):
                ring_state["rel"] += 1
                ring_issue()

        rot = {"T": 0, "P": 0, "P2": 0, "F": 0, "bank": 0, "S": 0, "O": 0, "PT": 0, "KO": 0}

        def nxt(k, n):
            v = rot[k]
            rot[k] = (v + 1) % n
            return v

        def norm_j(j):
            sch.op("act", lambda e, j=j: e.activation(out=regBf, in_=x[:, j, :], func=AF.Square,
                                                      accum_out=ssq[:, j:j + 1]),
                   reads=[("x", j)], writes=[("B", 0), ("B", 1), ("ssq", j)])
            sch.op("act", lambda e, j=j: e.activation(out=lnv[:, j:j + 1], in_=ssq[:, j:j + 1], func=AF.Ln,
                                                      scale=1.0 / D, bias=eps_t[:, 0:1]),
                   reads=[("ssq", j)], writes=[("lnv", j)])
            sch.op("act", lambda e, j=j: e.activation(out=rstd[:, j:j + 1], in_=lnv[:, j:j + 1], func=AF.Exp, scale=-0.5),
                   reads=[("lnv", j)], writes=[("rstd", j)])

        def scale_h(tl, j):
            sch.op("act", lambda e, j=j: e.activation(out=h[:, j, :], in_=x[:, j, :], func=AF.Copy,
                                                      scale=rstd[:, j:j + 1]),
                   reads=[("x", j), ("rstd", j)], writes=[("h", j)])

        def transposes(j):
            tb = nxt("T", 2)

            def fn(e, j=j, tb=tb):
                ins = None
                for k in range(8):
                    ins = e.transpose(out=T[tb][:, k * 128:(k + 1) * 128], in_=h[:, j, k * 128:(k + 1) * 128],
                                      identity=ident)
                return ins
            sch.op("pe", fn, reads=[("h", j)], writes=[("T", tb)])
            return tb

        def evac_gain(tb, j, dst, dkeys, gi):
            sch.op("dve", lambda e, j=j, tb=tb: e.tensor_tensor(
                out=dst[:, :, j * 128:(j + 1) * 128], in0=T[tb][:, :].rearrange("p (k t) -> p k t", t=128),
                in1=gcols[:, gi * 8:(gi + 1) * 8].unsqueeze(2).to_broadcast([128, 8, 128]), op=ALU.mult),
                reads=[("T", tb)], writes=dkeys)

        def ffn(tl, l):
            NT = tl["NT"]; ntok = NT * 128
            for j in range(NT):
                norm_j(j)
                scale_h(tl, j)
            for j in range(NT):
                tb = transposes(j)
                evac_gain(tb, j, hTa, [("hTa", j)], 0 if l == 0 else 3)
            for b in range(11):
                s = ring_take("gu")
                for fl in range(2):
                    f = 2 * b + fl
                    n = nxt("P", 3)

                    def fng(e, s=s, fl=fl, n=n):
                        ins = None
                        for k in range(8):
                            ins = e.matmul(P[n][:, 0:ntok], lhsT=ring[:, s, k * 256 + fl * 128:k * 256 + fl * 128 + 128],
                                           rhs=hTa[:, k, 0:ntok], start=(k == 0), stop=(k == 7))
                        return ins

                    def fnu(e, s=s, fl=fl, n=n):
                        ins = None
                        for k in range(8):
                            ins = e.matmul(P[n][:, 512:512 + ntok],
                                           lhsT=ring[:, s, 2048 + k * 256 + fl * 128:2048 + k * 256 + fl * 128 + 128],
                                           rhs=hTa[:, k, 0:ntok], start=(k == 0), stop=(k == 7))
                        return ins
                    hk = [("hTa", j) for j in range(NT)]
                    sch.op("pe", fng, reads=[("ring", s)] + hk, writes=[("P", n, 0)])
                    sch.op("pe", fnu, reads=[("ring", s)] + hk, writes=[("P", n, 1)])
                    r = f % 2
                    sch.op("act", lambda e, n=n, r=r: e.activation(out=regB[:, r, 0:ntok], in_=P[n][:, 0:ntok], func=AF.Silu),
                           reads=[("P", n, 0)], writes=[("B", r)])
                    sch.op("dve", lambda e, n=n, r=r, f=f: e.tensor_tensor(out=aT[:, f, 0:ntok], in0=regB[:, r, 0:ntok],
                                                                         in1=P[n][:, 512:512 + ntok], op=ALU.mult),
                           reads=[("B", r), ("P", n, 1)], writes=[A(f, j) for j in range(NT)])
                ring_release()
            for dh in range(2):
                slots = [ring_take("dn") for _ in range(3)]
                for j in range(NT):
                    n = nxt("bank", 6)
                    pn, half = n // 2, n % 2

                    def fnd(e, j=j, pn=pn, half=half, slots=slots):
                        ins = None
                        for f in range(NF):
                            s = slots[f // 8]
                            fo = (f % 8) * 512
                            ins = e.matmul(P[pn][:, half * 512:half * 512 + 512], lhsT=aT[:, f, j * 128:(j + 1) * 128],
                                           rhs=ring[:, s, fo:fo + 512], start=(f == 0), stop=(f == NF - 1))
                        return ins
                    sch.op("pe", fnd, reads=[("ring", s) for s in slots] + [A(f, j) for f in range(NF)],
                           writes=[("P", pn, half)])
                    sch.op("dve", lambda e, j=j, pn=pn, half=half, dh=dh: e.tensor_tensor(
                        out=x[:, j, dh * 512:(dh + 1) * 512], in0=x[:, j, dh * 512:(dh + 1) * 512],
                        in1=P[pn][:, half * 512:half * 512 + 512], op=ALU.add),
                        reads=[("P", pn, half), ("x", j)], writes=[("x", j)])
                    if l == 1 and dh == 1:
                        store_y(tl, j)
                ring_release(3)

        def store_y(tl, j):
            if tl["kind"] == "p":
                dst = y_p[tl["seq"], tl["t0"] + j * 128:tl["t0"] + (j + 1) * 128, :]
            else:
                dst = y_s.rearrange("b s d -> (b s) d")
            sch.dma("pool", "d_y%d" % j, dst, x[:, j, :], reads=[("x", j)], writes=[])
            nt = tl.get("next")
            if nt is not None and j < nt["NT"]:
                load_x(nt, j)

        def load_x(tl, j):
            if tl["kind"] == "p":
                src = xp[tl["seq"], tl["t0"] + j * 128:tl["t0"] + (j + 1) * 128, :]
            else:
                src = xs.rearrange("b s d -> (b s) d")
            sch.dma("sp", "d_x%d" % j, x[:, j, :], src, writes=[("x", j)])

        def pool_layer(tl):
            NT = tl["NT"]
            smp = tl["kind"] == "s"
            for j in range(NT):
                norm_j(j)
                sch.op("dve", lambda e, j=j: e.scalar_tensor_tensor(out=h[:, j, :], in0=x[:, j, :], scalar=rstd[:, j:j + 1],
                                                                    in1=gmix0_bc, op0=ALU.mult, op1=ALU.mult),
                       reads=[("x", j), ("rstd", j)], writes=[("h", j)])
                if tl["last"] and j == NT - 1:
                    fi = nxt("F", 3)
                    sch.op("dve", lambda e, j=j, fi=fi: e.scalar_tensor_tensor(out=Fp[:, fi, :], in0=x[:, j, :],
                                                                               scalar=rstd[:, j:j + 1], in1=gmix0_bc,
                                                                               op0=ALU.mult, op1=ALU.mult),
                           reads=[("x", j), ("rstd", j)], writes=[("F", fi)])
                    if smp:
                        sch.dma("pool", "d_F%d" % fi, ps_s[0, :, :], Fp[49:64, fi, :], reads=[("F", fi)])
                        sch.dma("pool", "d_F%d" % fi, ps_s[1, :, :], Fp[113:128, fi, :], reads=[("F", fi)])
                    else:
                        sch.dma("pool", "d_F%d" % fi, ps_p[tl["seq"], :, :], Fp[113:128, fi, :], reads=[("F", fi)])
            sp_ = ring_take("wp")
            for j in range(NT):
                n = nxt("P", 3)
                first = tl["first"] and j == 0

                def fnp(e, j=j, n=n, first=first):
                    ins = None
                    for c in range(8):
                        g = c // 2
                        o = P[n][:, c * 128:(c + 1) * 128]
                        lt = h[:, j, c * 128:(c + 1) * 128]
                        if smp:
                            e.matmul(o, lhsT=lt, rhs=band(12 + g), start=True, stop=False)
                            ins = e.matmul(o, lhsT=spool[0:32, c * 128:(c + 1) * 128], rhs=cbf[0:32, 2 + 16 + g, :],
                                           start=False, stop=True)
                        elif first:
                            ins = e.matmul(o, lhsT=lt, rhs=band(8 + g), start=True, stop=True)
                        else:
                            e.matmul(o, lhsT=lt, rhs=band(g), start=True, stop=False)
                            hp = hprev[:, c * 128:(c + 1) * 128] if j == 0 else h[:, j - 1, c * 128:(c + 1) * 128]
                            ins = e.matmul(o, lhsT=hp, rhs=band(4 + g), start=False, stop=True)
                    return ins
                rds = [("h", j)]
                if smp:
                    rds.append("spool")
                elif not first:
                    rds.append("hprev" if j == 0 else ("h", j - 1))
                sch.op("pe", fnp, reads=rds, writes=[("P", n, 0), ("P", n, 1)])
                sch.op("act", lambda e, j=j, n=n: e.activation(out=hTa[:, :, j * 128:(j + 1) * 128],
                                                               in_=P[n][:, :].rearrange("p (c t) -> p c t", t=128), func=AF.Copy),
                       reads=[("P", n, 0), ("P", n, 1)], writes=[("hTa", j)])
            if not smp and not tl["last"]:
                sch.op("dve", lambda e: e.tensor_copy(out=hprev[:, :], in_=h[:, NT - 1, :]),
                       reads=[("h", NT - 1)], writes=["hprev"])
            for j in range(NT):
                n = nxt("P", 3)

                def fnw(e, j=j, n=n):
                    ins = None
                    for g in range(4):
                        for cc in range(2):
                            c = 2 * g + cc
                            ins = e.matmul(P[n][:, g * 256:(g + 1) * 256], lhsT=hTa[:, c, j * 128:(j + 1) * 128],
                                           rhs=ring[:, sp_, c * 256:(c + 1) * 256], start=(cc == 0), stop=(cc == 1))
                    return ins
                sch.op("pe", fnw, reads=[("hTa", j), ("ring", sp_)], writes=[("P", n, 0), ("P", n, 1)])
                fi = nxt("F", 3)
                sch.op("dve", lambda e, n=n, fi=fi: e.tensor_tensor(out=Fp[:, fi, :], in0=P[n][:, :], in1=pscale_bc, op=ALU.mult),
                       reads=[("P", n, 0), ("P", n, 1)], writes=[("F", fi)])
                sch.op("dve", lambda e, j=j, fi=fi: e.tensor_tensor(out=x[:, j, :], in0=x[:, j, :], in1=Fp[:, fi, :], op=ALU.add),
                       reads=[("F", fi), ("x", j)], writes=[("x", j)])
            ring_release()

        def qk_transposes(jl, dst, dname):
            for j in jl:
                for hg in range(2):
                    tb = nxt("T", 2)

                    def fnt(e, j=j, hg=hg, tb=tb):
                        ins = None
                        for hh in range(8):
                            ins = e.transpose(out=T[tb][0:68, hh * 128:(hh + 1) * 128], in_=qk_aug[:, j, hg * 8 + hh, 0:68],
                                              identity=ident)
                        return ins
                    sch.op("pe", fnt, reads=[("QK", j)], writes=[("T", tb)])
                    if hg == 0:
                        sch.op("dve", lambda e, j=j, hg=hg, tb=tb: e.tensor_copy(
                            out=dst[0:68, hg * 8:(hg + 1) * 8, j * 128:(j + 1) * 128],
                            in_=T[tb][0:68, :].rearrange("p (a t) -> p a t", t=128)),
                            reads=[("T", tb)], writes=[(dname, hg, j)])
                    else:
                        sch.op("act", lambda e, j=j, hg=hg, tb=tb: e.activation(
                            out=dst[0:68, hg * 8:(hg + 1) * 8, j * 128:(j + 1) * 128],
                            in_=T[tb][0:68, :].rearrange("p (a t) -> p a t", t=128), func=AF.Copy),
                            reads=[("T", tb)], writes=[(dname, hg, j)])

        def headnorm(n, gbcast, out_ap, out_keys, j, extra_reads=()):
            sch.op("act", lambda e, n=n: e.activation(out=regBf, in_=P[n][:, :], func=AF.Square),
                   reads=[("P", n, 0), ("P", n, 1)], writes=[("B", 0), ("B", 1)])
            sch.op("dve", lambda e: e.tensor_reduce(out=ssh[:, :], in_=regBf.rearrange("p (a d) -> p a d", d=64),
                                                    axis=AX.X, op=ALU.add),
                   reads=[("B", 0), ("B", 1)], writes=["ssh"])
            sch.op("act", lambda e: e.activation(out=lnh[:, :], in_=ssh[:, :], func=AF.Ln, scale=1.0 / DH, bias=eps_t[:, 0:1]),
                   reads=["ssh"], writes=["lnh"])
            sch.op("act", lambda e: e.activation(out=rsh[:, :], in_=lnh[:, :], func=AF.Exp, scale=-0.5),
                   reads=["lnh"], writes=["rsh"])
            pv_ = P[n][:, :].rearrange("p (a d) -> p a d", d=64)
            rb = rsh[:, :].unsqueeze(2).to_broadcast([128, NH, DH])
            if out_ap is not None:
                sch.op("dve", lambda e: e.tensor_tensor(out=out_ap, in0=pv_, in1=rb, op=ALU.mult),
                       reads=[("P", n, 0), ("P", n, 1), "rsh"], writes=out_keys)
                return None
            fi = nxt("F", 3)
            fv = Fp[:, fi, :].rearrange("p (a d) -> p a d", d=64)
            sch.op("dve", lambda e: e.tensor_tensor(out=fv, in0=pv_, in1=rb, op=ALU.mult),
                   reads=[("P", n, 0), ("P", n, 1), "rsh"], writes=[("F", fi)])
            sch.op("dve", lambda e: e.tensor_tensor(out=fv, in0=fv, in1=gbcast.unsqueeze(1).to_broadcast([128, NH, DH]), op=ALU.mult),
                   reads=[("F", fi)], writes=[("F", fi)])
            return fi

        def proj(tl, j, src, skeys, kind):
            n = nxt("P2", 2)
            for hf in range(2):
                s = tl["slots"][kind][hf]

                def fn(e, s=s, hf=hf, n=n, j=j):
                    ins = None
                    for k in range(8):
                        ins = e.matmul(P[n][:, hf * 512:(hf + 1) * 512], lhsT=src[:, k, j * 128:(j + 1) * 128],
                                       rhs=ring[:, s, k * 512:(k + 1) * 512], start=(k == 0), stop=(k == 7))
                    return ins
                sch.op("pe", fn, reads=[("ring", s)] + skeys(j), writes=[("P", n, hf)])
            return n

        def tok_rows(tl, j, dram_p, dram_s):
            if tl["kind"] == "p":
                return dram_p[tl["seq"], tl["t0"] + j * 128:tl["t0"] + (j + 1) * 128, :]
            return dram_s.rearrange("b s d -> (b s) d")

        def kvq(tl):
            NT = tl["NT"]
            smp = tl["kind"] == "s"
            kc0 = tl["t0"] // 128
            for j in range(NT):
                norm_j(j)
                scale_h(tl, j)
            for j in range(NT):
                tb = transposes(j)
                evac_gain(tb, j, hTb, [A(k, j) for k in range(8)], 1)
                evac_gain(tb, j, hTa, [("hTa", j)], 2)
            tl["slots"] = {}
            hb = lambda j: [A(k, j) for k in range(8)]
            ha = lambda j: [("hTa", j)]
            for j in range(NT):
                def fnz(e, j=j):
                    ins = None
                    for k in range(8):
                        ins = e.matmul(P[2][:, j * NH:(j + 1) * NH], lhsT=hTb[:, k, j * 128:(j + 1) * 128], rhs=wf[:, k, :],
                                       start=(k == 0), stop=(k == 7))
                    return ins
                sch.op("pe", fnz, reads=hb(j) + ["wf"], writes=[("P", 2, 0)])
            for j in range(NT):
                sch.op("dve", lambda e, j=j: e.tensor_tensor(out=zb[:, :], in0=P[2][:, j * NH:(j + 1) * NH], in1=bf_bc, op=ALU.add),
                       reads=[("P", 2, 0)], writes=["zb"])
                sch.op("act", lambda e: e.activation(out=az[:, :], in_=zb[:, :], func=AF.Abs),
                       reads=["zb"], writes=["az"])
                sch.op("act", lambda e: e.activation(out=ez[:, :], in_=az[:, :], func=AF.Exp, scale=-1.0),
                       reads=["az"], writes=["ez"])
                sch.op("act", lambda e: e.activation(out=lz[:, :], in_=ez[:, :], func=AF.Ln, bias=one_t[:, 0:1]),
                       reads=["ez"], writes=["lz"])
                sch.op("dve", lambda e: e.tensor_scalar(out=mz[:, :], in0=zb[:, :], scalar1=0.0, scalar2=None, op0=ALU.min),
                       reads=["zb"], writes=["mz"])
                sch.op("dve", lambda e, j=j: e.tensor_tensor(out=lf[:, j, :], in0=mz[:, :], in1=lz[:, :], op=ALU.subtract),
                       reads=["mz", "lz"], writes=[("lf", j)])
                if smp:
                    ldst = lf_s.rearrange("b s c -> (b s) c")
                else:
                    ldst = lf_p[tl["seq"], tl["t0"] + j * 128:tl["t0"] + (j + 1) * 128, :]
                sch.dma("pool", "d_lf%d" % j, ldst, lf[:, j, :], reads=[("lf", j)])
            tl["slots"]["wk"] = [ring_take("wk") for _ in range(2)]
            sch.op("dve", lambda e: e.memset(qk_aug[:, :, :, 64], 1.0), writes=[("QK", j) for j in range(4)])
            for j in range(NT):
                n = proj(tl, j, hTb, hb, "wk")
                fi = headnorm(n, gk_bc, None, None, j)
                sch.dma("pool", "d_F%d" % fi, tok_rows(tl, j, k_p, k_s), Fp[:, fi, :], reads=[("F", fi)])
                sch.op("dve", lambda e, j=j, fi=fi: e.tensor_tensor(out=qk_aug[:, j, :, 0:64],
                                                                     in0=Fp[:, fi, :].rearrange("p (a d) -> p a d", d=64),
                                                                     in1=gq_bc.unsqueeze(1).to_broadcast([128, NH, DH]), op=ALU.mult),
                       reads=[("F", fi)], writes=[("QK", j)])
            ring_release(2)
            for j in range(NT):
                cum_pe(lf[:, j, :], [("lf", j)], smp, j)
            for j in range(NT):
                cum_dve(DprevS if smp else Dprev, "DprevS" if smp else "Dprev",
                        negD[:, kc0 + j, :], [("ND", kc0 + j)], Dt[:, j, :], [("Dt", j)], j)
            nd_pieces(negD[:, kc0:kc0 + NT, :], [("ND", kc0 + j) for j in range(NT)], NT)
            tl["slots"]["wv"] = [ring_take("wv") for _ in range(2)]
            for j in range(NT):
                n = proj(tl, j, hTb, hb, "wv")
                fi = nxt("F", 3)
                sch.op("act", lambda e, n=n, fi=fi: e.activation(out=Fp[:, fi, :], in_=P[n][:, :], func=AF.Copy),
                       reads=[("P", n, 0), ("P", n, 1)], writes=[("F", fi)])
                sch.dma("pool", "d_F%d" % fi, tok_rows(tl, j, v_p, v_s), Fp[:, fi, :], reads=[("F", fi)])

                def fnv(e, j=j, fi=fi):
                    fv = Fp[:, fi, :].rearrange("p (a b d) -> p a b d", b=2, d=64)
                    e.activation(out=Vaug[:, j, :, 0:64], in_=fv[:, :, 0, :], func=AF.Copy)
                    return e.activation(out=Vaug[:, j, :, 128:192], in_=fv[:, :, 1, :], func=AF.Copy)
                sch.op("act", fnv, reads=[("F", fi)], writes=[("VA", j)])
            store_v_tile(tl)
            ring_release(2)
            qk_transposes(range(NT), KT, "KT")
            store_kt(tl, NT * 128)
            sch.op("dve", lambda e: e.memset(qk_aug[:, :, :, 65:68], 1.0), writes=[("QK", j) for j in range(4)])
            tl["slots"]["wq"] = [ring_take("wq") for _ in range(2)]
            for j in range(NT):
                n = proj(tl, j, hTa, ha, "wq")
                headnorm(n, gq_bc, qk_aug[:, j, :, 0:64], [("QK", j)], j)
                sch.op("dve", lambda e, j=j: e.tensor_scalar(out=qk_aug[:, j, :, 64], in0=Dt[:, j, :], scalar1=8.0,
                                                             scalar2=None, op0=ALU.mult),
                       reads=[("Dt", j)], writes=[("QK", j)])
            ring_release(2)
            qk_transposes(range(NT), QT, "QT")

        def nd_pieces(nd_src, nd_keys, n):
            qkk = [("QK", j) for j in range(n)]
            A_ = pcA[:, 0:n, :]
            B_ = pcB[:, 0:n, :]
            c65, c66, c67 = qk_aug[:, 0:n, :, 65], qk_aug[:, 0:n, :, 66], qk_aug[:, 0:n, :, 67]
            sch.op("dve", lambda e: e.tensor_scalar(out=A_, in0=nd_src, scalar1=8.0, scalar2=None, op0=ALU.mult),
                   reads=nd_keys, writes=["pcA"])
            sch.op("dve", lambda e: e.tensor_copy(out=c65, in_=A_), reads=["pcA"], writes=qkk)
            sch.op("dve", lambda e: e.tensor_tensor(out=B_, in0=A_, in1=c65, op=ALU.subtract), reads=["pcA"] + qkk, writes=["pcB"])
            sch.op("dve", lambda e: e.tensor_copy(out=c66, in_=B_), reads=["pcB"], writes=qkk)
            sch.op("dve", lambda e: e.tensor_tensor(out=A_, in0=B_, in1=c66, op=ALU.subtract), reads=["pcB"] + qkk, writes=["pcA"])
            sch.op("dve", lambda e: e.tensor_copy(out=c67, in_=A_), reads=["pcA"], writes=qkk)

        def cum_pe(lsrc, lkeys, smp, co):
            ui = 2 if smp else 0
            c0 = 512 + co * 2 * NH

            def fnc(e):
                e.matmul(P[2][:, c0:c0 + NH], lhsT=cf[:, ui, :], rhs=lsrc, start=True, stop=True)
                return e.matmul(P[2][:, c0 + NH:c0 + 2 * NH], lhsT=cf[:, ui + 1, :], rhs=lsrc, start=True, stop=True)
            sch.op("pe", fnc, reads=lkeys, writes=[("P", 2, 1)])

        def cum_dve(dprev, dpkey, nd_out, nd_keys, dt_out, dt_keys, co):
            c0 = 512 + co * 2 * NH
            rk = [("P", 2, 1), dpkey]
            if dt_out is not None:
                sch.op("dve", lambda e: e.tensor_tensor(out=dt_out, in0=P[2][:, c0:c0 + NH], in1=dprev[:, :], op=ALU.add),
                       reads=rk, writes=dt_keys)
            sch.op("dve", lambda e: e.scalar_tensor_tensor(out=nd_out, in0=P[2][:, c0:c0 + NH], scalar=-1.0, in1=dprev[:, :],
                                                           op0=ALU.mult, op1=ALU.subtract),
                   reads=rk, writes=nd_keys)
            sch.op("dve", lambda e: e.tensor_tensor(out=dprev[:, :], in0=dprev[:, :], in1=P[2][:, c0 + NH:c0 + 2 * NH], op=ALU.add),
                   reads=rk, writes=[dpkey])

        ing_keys = []

        def store_kt(tl, nk, slot=None, k0=None, eng="pool"):
            if tl is not None and tl["kind"] == "s":
                return
            if slot is None:
                slot, k0 = tl["seq"], tl["t0"]
                sem = "d_kt%d" % ((k0 // 512) % 2)
            else:
                sem = "d_kti"
                ing_keys.append(("KTs", slot, k0 // 512))
            return sch.dma(eng, sem, KTs[slot, :, :, k0:k0 + nk].rearrange("h r t -> r h t"), KT[0:68, :, 0:nk],
                           reads=[("KT", hg, j) for hg in range(2) for j in range(nk // 128)],
                           writes=[("KTs", slot, k0 // 512)])

        def store_v_tile(tl, slot=None, k0=None, eng="pool"):
            if tl is not None and tl["kind"] == "s":
                return
            if slot is None:
                slot, k0 = tl["seq"], tl["t0"]
                sem = "d_vs%d" % ((k0 // 512) % 2)
            else:
                sem = "d_vsi"
            tok = None
            for j in range(4):
                tok = sch.dma(eng, sem + "_%d" % j, Vs[slot, :, k0 + j * 128:k0 + (j + 1) * 128, :].rearrange("a t c -> t a c"),
                              Vaug[:, j, :, :], reads=[("VA", j)], writes=[("Vs", slot, k0 // 128 + j)])
            return tok

        def attend(tl):
            NT = tl["NT"]
            smp = tl["kind"] == "s"
            ntok = NT * 128
            items = []
            loads = []
            LA = 2

            def S_bank():
                sb_ = nxt("S", 3)
                return P[sb_ // 2][:, (sb_ % 2) * 512:(sb_ % 2) * 512 + 512], ("P", sb_ // 2, sb_ % 2)

            def mk_load(slot, hd, blk, nkc, lidx):
                pair, voff = hd // 2, (0 if hd % 2 == 0 else 64)

                def emit():
                    ko = lidx % 3
                    sch.dma("sp", "d_ko%d" % ko, KTold[0:68, ko, 0:nkc * 128],
                            KTs[slot, hd, :, blk * 1024:blk * 1024 + nkc * 128],
                            reads=[("KTs", slot, blk * 2), ("KTs", slot, blk * 2 + 1)], writes=[("KO", ko)])
                    sch.dma("sp", "d_vo%d" % ko, Vold[:, ko, 0:nkc, :],
                            Vs[slot, pair, blk * 1024:blk * 1024 + nkc * 128, voff:voff + 128].rearrange("(a p) c -> p a c", p=128),
                            reads=[("Vs", slot, blk * 8 + a_) for a_ in range(nkc)], writes=[("VO", ko)])
                return emit

            def mk_old(hd, psO, okey, lidx, a_, q0, nq, nd, ndk, st, sp_, pre):
                ko = lidx % 3
                it = {"pre": pre}

                def qk(it):
                    it["psS"], it["skey"] = S_bank()
                    sch.op("pe", lambda e, psS=it["psS"]: e.matmul(
                        psS[:, q0:q0 + nq], lhsT=KTold[0:68, ko, a_ * 128:(a_ + 1) * 128], rhs=QT[0:68, hd, q0:q0 + nq],
                        start=True, stop=True),
                        reads=[("KO", ko)] + [("QT", hd // 8, j) for j in range(NT)], writes=[it["skey"]])

                def ex(it):
                    it["pr"] = nxt("PT", 4)
                    sch.op("act", lambda e, psS=it["psS"], pr=it["pr"]: e.activation(
                        out=PT[:, pr, q0:q0 + nq], in_=psS[:, q0:q0 + nq], func=AF.Exp, scale=0.125),
                        reads=[it["skey"]] + ndk, writes=[("PT", it["pr"])])

                def pv(it):
                    sch.op("pe", lambda e, pr=it["pr"]: e.matmul(
                        psO[:, q0:q0 + nq], lhsT=Vold[:, ko, a_, :], rhs=PT[:, pr, q0:q0 + nq], start=st, stop=sp_),
                        reads=[("VO", ko), ("PT", it["pr"])], writes=[okey])
                it.update(qk=qk, ex=ex, pv=pv)
                return it

            def mk_diag(hd, psO, okey, dl, nq, st, sp_):
                pair, voff = hd // 2, (0 if hd % 2 == 0 else 64)
                kcg = tl["t0"] // 128 + dl
                c0 = dl * 128
                nqq = nq - c0
                it = {"pre": []}

                def qk(it):
                    it["psS"], it["skey"] = S_bank()

                    def fnq(e, psS=it["psS"]):
                        e.matmul(psS[:, c0:c0 + nqq], lhsT=KT[0:68, hd, dl * 128:(dl + 1) * 128],
                                 rhs=QT[0:68, hd, c0:c0 + nqq], start=True, stop=False)
                        return e.matmul(psS[:, c0:c0 + 128], lhsT=ident, rhs=masktri, start=False, stop=True)
                    sch.op("pe", fnq, reads=[("KT", hd // 8, dl)] + [("QT", hd // 8, j) for j in range(NT)], writes=[it["skey"]])

                def ex(it):
                    it["pr"] = nxt("PT", 4)
                    sch.op("act", lambda e, psS=it["psS"], pr=it["pr"]: e.activation(
                        out=PT[:, pr, c0:c0 + nqq], in_=psS[:, c0:c0 + nqq], func=AF.Exp, scale=0.125),
                        reads=[it["skey"], ("ND", kcg)], writes=[("PT", it["pr"])])

                def pv(it):
                    sch.op("pe", lambda e, pr=it["pr"]: e.matmul(
                        psO[:, c0:c0 + nqq], lhsT=Vaug[:, dl, pair, voff:voff + 128], rhs=PT[:, pr, c0:c0 + nqq],
                        start=st, stop=sp_),
                        reads=[("VA", dl), ("PT", it["pr"])], writes=[okey])
                it.update(qk=qk, ex=ex, pv=pv)
                return it

            def mk_diag_s(hd, psO, okey, q, st, sp_):
                pair, voff = hd // 2, (0 if hd % 2 == 0 else 64)
                pb = q * 64
                kcs = past // 128
                it = {"pre": []}

                def qk(it):
                    it["psS"], it["skey"] = S_bank()

                    def fnq(e, psS=it["psS"]):
                        e.matmul(psS[pb:pb + 64, pb:pb + 64], lhsT=KT[0:68, hd, pb:pb + 64],
                                 rhs=QT[0:68, hd, pb:pb + 64], start=True, stop=False)
                        return e.matmul(psS[pb:pb + 64, pb:pb + 64], lhsT=cbf[pb:pb + 64, 0, pb:pb + 64],
                                        rhs=cbf[pb:pb + 64, 1, pb:pb + 64], start=False, stop=True)
                    sch.op("pe", fnq, reads=[("KT", hd // 8, 0), ("QT", hd // 8, 0)], writes=[it["skey"]])

                def ex(it):
                    it["pr"] = nxt("PT", 4)
                    sch.op("act", lambda e, psS=it["psS"], pr=it["pr"]: e.activation(
                        out=PT[pb:pb + 64, pr, pb:pb + 64], in_=psS[pb:pb + 64, pb:pb + 64], func=AF.Exp, scale=0.125),
                        reads=[it["skey"], ("ND", kcs)], writes=[("PT", it["pr"])])

                def pv(it):
                    sch.op("pe", lambda e, pr=it["pr"]: e.matmul(
                        psO[:, pb:pb + 64], lhsT=Vaug[pb:pb + 64, 0, pair, voff:voff + 128],
                        rhs=PT[pb:pb + 64, pr, pb:pb + 64], start=st, stop=sp_),
                        reads=[("VA", 0), ("PT", it["pr"])], writes=[okey])
                it.update(qk=qk, ex=ex, pv=pv)
                return it

            def mk_norm(hd, psO, okey):
                pair, par = hd // 2, hd % 2
                o0, d0 = (0, 64) if par == 0 else (64, 0)

                def emit():
                    fi = nxt("F", 3)
                    sch.op("dve", lambda e: e.reciprocal(out=Fp[o0:o0 + 64, fi, 0:ntok], in_=psO[d0:d0 + 64, 0:ntok]),
                           reads=[okey], writes=[("F", fi)])
                    sch.op("dve", lambda e: e.tensor_tensor(out=OT[o0:o0 + 64, pair, 0:ntok], in0=psO[o0:o0 + 64, 0:ntok],
                                                           in1=Fp[o0:o0 + 64, fi, 0:ntok], op=ALU.mult),
                           reads=[okey, ("F", fi)], writes=[A(8 + pair, j) for j in range(NT)])
                return emit

            def ensure(upto):
                while loads and loads[0][0] <= upto:
                    loads.pop(0)[1]()

            nload = [0]
            for hd in range(NH):
                ob = nxt("O", 2)
                psO = P[1][:, 512:1024] if ob == 0 else P[2][:, 0:512]
                okey = ("P", 1, 1) if ob == 0 else ("P", 2, 0)
                if smp:
                    segs = [(q * 64, 64, [("old", 2 + q, blk, 8, q) for blk in range(past // 1024)] + [("diag_s", q)])
                            for q in range(2)]
                else:
                    nold = tl["t0"]
                    chunks = [("old", tl["seq"], blk, min(8, (nold - blk * 1024) // 128), None)
                              for blk in range((nold + 1023) // 1024)]
                    chunks += [("diag", dl) for dl in range(NT)]
                    segs = [(0, ntok, chunks)]
                last_item = None
                for (q0, nq, chunks) in segs:
                    nch = sum(c[3] if c[0] == "old" else 1 for c in chunks)
                    done = 0
                    for c in chunks:
                        if c[0] == "old":
                            _, slot, blk, nkc, q = c
                            lidx = nload[0]
                            nload[0] += 1
                            loads.append((lidx, mk_load(slot, hd, blk, nkc, lidx)))
                            for a_ in range(nkc):
                                kcg = blk * 8 + a_
                                if smp:
                                    nd = negDs[:, q * (past // 128) + kcg, hd:hd + 1]
                                    ndk = [("NDS", q)]
                                else:
                                    nd = negD[:, kcg, hd:hd + 1]
                                    ndk = [("ND", kcg)]
                                done += 1
                                pre = [(lambda l=lidx: ensure(l + 1))] if a_ == 0 else []
                                last_item = mk_old(hd, psO, okey, lidx, a_, q0, nq, nd, ndk, done == 1, done == nch, pre)
                                items.append(last_item)
                        elif c[0] == "diag":
                            done += 1
                            last_item = mk_diag(hd, psO, okey, c[1], nq, done == 1, done == nch)
                            items.append(last_item)
                        else:
                            done += 1
                            last_item = mk_diag_s(hd, psO, okey, c[1], done == 1, done == nch)
                            items.append(last_item)
                last_item["post"] = mk_norm(hd, psO, okey)
            for i in range(len(items) + LA):
                if i < len(items):
                    it = items[i]
                    for f in it["pre"]:
                        f()
                    it["qk"](it)
                    it["ex"](it)
                if i >= LA:
                    it = items[i - LA]
                    it["pv"](it)
                    if "post" in it:
                        it["post"]()

        def wo_proj(tl):
            NT = tl["NT"]
            slots = [ring_take("wo") for _ in range(2)]
            for j in range(NT):
                for dh in range(2):
                    n = nxt("bank", 6)
                    pn, half = n // 2, n % 2
                    s = slots[dh]

                    def fno(e, j=j, pn=pn, half=half, s=s):
                        ins = None
                        for pr in range(8):
                            ins = e.matmul(P[pn][:, half * 512:half * 512 + 512], lhsT=OT[:, pr, j * 128:(j + 1) * 128],
                                           rhs=ring[:, s, pr * 512:(pr + 1) * 512], start=(pr == 0), stop=(pr == 7))
                        return ins
                    sch.op("pe", fno, reads=[("ring", s)] + [A(8 + pr, j) for pr in range(8)], writes=[("P", pn, half)])
                    sch.op("dve", lambda e, j=j, pn=pn, half=half, dh=dh: e.tensor_tensor(
                        out=x[:, j, dh * 512:(dh + 1) * 512], in0=x[:, j, dh * 512:(dh + 1) * 512],
                        in1=P[pn][:, half * 512:half * 512 + 512], op=ALU.add),
                        reads=[("P", pn, half), ("x", j)], writes=[("x", j)])
            ring_release(2)

        def ingest():
            for q in range(2):
                slot = 2 + q
                sch.dma("sp", "d_lfc", lfc[:, :, :], cl[q].rearrange("(a p) c -> p a c", p=128), writes=["lfc"])
                sch.op("dve", lambda e: e.memset(Dprev[:, :], 0.0), writes=["Dprev"])
                for kg in range(past // 512):
                    for jj in range(4):
                        cum_pe(lfc[:, kg * 4 + jj, :], ["lfc"], False, jj)
                    for jj in range(4):
                        cum_dve(Dprev, "Dprev", negDs[:, q * (past // 128) + kg * 4 + jj, :], [("NDS", q)], None, None, jj)
                sch.op("dve", lambda e, q=q: e.tensor_copy(out=DprevS[q * 64:(q + 1) * 64, :], in_=Dprev[q * 64:(q + 1) * 64, :]),
                       reads=["Dprev"], writes=["DprevS"])
                for blk in range(past // 512):
                    for jj in range(4):
                        kc = blk * 4 + jj
                        rows = slice(kc * 128, (kc + 1) * 128)
                        fi = nxt("F", 3)
                        sch.dma("sp", "d_ingk%d" % jj, Fp[:, fi, :], ck[q, rows, :], writes=[("F", fi)])
                        sch.op("dve", lambda e, jj=jj, fi=fi: e.tensor_tensor(
                            out=qk_aug[:, jj, :, 0:64], in0=Fp[:, fi, :].rearrange("p (a d) -> p a d", d=64),
                            in1=gq_bc.unsqueeze(1).to_broadcast([128, NH, DH]), op=ALU.mult),
                            reads=[("F", fi)], writes=[("QK", jj)])
                        fi2 = nxt("F", 3)
                        sch.dma("sp", "d_ingv%d" % jj, Fp[:, fi2, :], cv[q, rows, :], writes=[("F", fi2)])

                        def fnv(e, jj=jj, fi2=fi2):
                            fv = Fp[:, fi2, :].rearrange("p (a b d) -> p a b d", b=2, d=64)
                            e.activation(out=Vaug[:, jj, :, 0:64], in_=fv[:, :, 0, :], func=AF.Copy)
                            return e.activation(out=Vaug[:, jj, :, 128:192], in_=fv[:, :, 1, :], func=AF.Copy)
                        sch.op("act", fnv, reads=[("F", fi2)], writes=[("VA", jj)])
                    sch.op("dve", lambda e: e.memset(qk_aug[:, :, :, 64], 1.0), writes=[("QK", j) for j in range(4)])
                    b0 = q * (past // 128) + blk * 4
                    nd_pieces(negDs[:, b0:b0 + 4, :], [("NDS", q)], 4)
                    qk_transposes(range(4), KT, "KT")
                    tk = store_kt(None, 512, slot, blk * 512, eng="sp")
                    tv = store_v_tile(None, slot, blk * 512, eng="sp")
            for kk in ing_keys:
                sch.res[kk]["w"] = tk
            sch.dma("pool", "d_spool", spool[0:15, :], spl[0, :, :], writes=["spool"])
            sch.dma("pool", "d_spool", spool[16:31, :], spl[1, :, :], writes=["spool"])

        eps_t = sb("eps_t", [128, 1], F32)
        one_t = sb("one_t", [128, 1], F32)
        sch.op("dve", lambda e: e.memset(eps_t[:, :], EPS), writes=["eps"])
        t1 = sch.op("dve", lambda e: e.memset(one_t[:, :], 1.0), writes=["one"])
        sch.wait_tok("act", t1)

        tiles = []
        for seq in range(2):
            for ti in range(S // 512):
                tiles.append({"kind": "p", "seq": seq, "t0": ti * 512, "NT": 4, "first": ti == 0,
                              "last": ti == S // 512 - 1})
        tiles.append({"kind": "s", "seq": 0, "t0": past, "NT": 1, "first": False, "last": True})
        for i in range(len(tiles) - 1):
            tiles[i]["next"] = tiles[i + 1]
        tiles[-1]["next"] = None

        for j in range(4):
            load_x(tiles[0], j)
        ingest()
        for _ in range(R_RING):
            ring_issue()
        for tl in tiles:
            if tl["kind"] == "p" and tl["first"]:
                sch.op("dve", lambda e: e.memset(Dprev[:, :], 0.0), writes=["Dprev"])
            pool_layer(tl)
            ffn(tl, 0)
            kvq(tl)
            attend(tl)
            wo_proj(tl)
            ffn(tl, 1)

        for sname, v in sch.cnt.items():
            if sname.startswith("d_"):
                sch.q["pool"].append(("wait", sname, v))

        sem_names = sorted(sch.cnt.keys())
        sems = {n: es.enter_context(nc.semaphore(n)) for n in sem_names}
        block = es.enter_context(nc.Block())

        def run(qname):
            def body(eng):
                for item in sch.q[qname]:
                    if item[0] == "wait":
                        eng.wait_ge(sems[item[1]], item[2])
                    else:
                        ins = item[1](eng)
                        ins.then_inc(sems[item[2]], item[3])
            return body

        block.tensor(run("pe"))
        block.scalar(run("act"))
        block.vector(run("dve"))
        block.gpsimd(run("pool"))
        block.sync(run("sp"))
    nc_sch_holder.append(sch)
    return nc


_CACHE = {}
nc_sch_holder = []


def _host_inputs(inp, c):
    f = lambda a: np.ascontiguousarray(np.asarray(a, dtype=np.float32))
    sl = slice(2 * c, 2 * c + 2)
    g_mix = np.asarray(inp["g_mix"], np.float32); g_ffn = np.asarray(inp["g_ffn"], np.float32)
    cols = lambda v: np.asarray(v, np.float32).reshape(8, 128).T
    gbc = np.concatenate([
        np.broadcast_to(g_mix[0][None, :], (128, D)),
        np.broadcast_to(np.asarray(inp["pool_scale"], np.float32)[0][None, :], (128, D)),
        np.broadcast_to(np.asarray(inp["g_qnorm"], np.float32)[0][None, :], (128, DH)),
        np.broadcast_to(np.asarray(inp["g_knorm"], np.float32)[None, :], (128, DH)),
        np.broadcast_to(np.asarray(inp["b_f"], np.float32)[None, :], (128, NH)),
        cols(g_ffn[0]), cols(inp["g_kv"]), cols(g_mix[1]), cols(g_ffn[1])], axis=1)
    B, S = inp["x_prompt"].shape[0], inp["x_prompt"].shape[1]
    past = inp["cache_k"].shape[1]
    return {
        "xp": f(inp["x_prompt"][sl]), "xs": f(inp["x_sample"][sl]), "spl": f(inp["state_pool"][0, sl]),
        "ck": f(np.asarray(inp["cache_k"][sl]).reshape(2, past, D)), "cv": f(np.asarray(inp["cache_v"][sl]).reshape(2, past, D)),
        "cl": f(inp["cache_logf"][sl]),
        "pool_w": f(np.asarray(inp["pool_w"])[0].reshape(D, 256)),
        "w_k": f(inp["w_k"]), "w_v": f(inp["w_v"]), "w_f": f(inp["w_f"]),
        "w_q": f(np.asarray(inp["w_q"])[0]), "w_o": f(np.asarray(inp["w_o"])[0]),
        "w_gate": f(inp["w_gate"]), "w_up": f(inp["w_up"]), "w_down": f(inp["w_down"]),
        "gbc": f(gbc),
    }


def kernel(**inputs):
    S = inputs["x_prompt"].shape[1]
    past = inputs["cache_k"].shape[1]
    ssl = inputs["x_sample"].shape[1]
    nb = inputs["x_prompt"].shape[0]
    ncores = nb // 2
    key = (S, past, ssl)
    if key not in _CACHE:
        _CACHE[key] = build_program(S, past, ssl)
    nc = _CACHE[key]
    cb, cf = _consts()
    in_maps = []
    for c in range(ncores):
        m = _host_inputs(inputs, c)
        m["cbf"] = cb
        m["cf32"] = cf
        in_maps.append(m)
    res = run_bass_kernel_spmd(nc, in_maps, core_ids=list(range(ncores)))
    rs = res.results
    cat = lambda n: np.concatenate([np.asarray(r[n], dtype=np.float32) for r in rs], axis=0)
    y_p = cat("y_p"); y_s = cat("y_s")
    ps_p = cat("ps_p")[None]; ps_s = cat("ps_s")[None]
    k_p = cat("k_p").reshape(nb, S, NH, DH); v_p = cat("v_p").reshape(nb, S, NH, DH); lf_p = cat("lf_p")
    k_s = cat("k_s").reshape(nb, ssl, NH, DH); v_s = cat("v_s").reshape(nb, ssl, NH, DH); lf_s = cat("lf_s")
    return (y_p, y_s, ps_p, ps_s, k_p, v_p, lf_p, k_s, v_s, lf_s)
```

```python
import contextlib
import numpy as np
import ml_dtypes
import concourse.bass as bass
import concourse.mybir as mybir
from concourse.bass_utils import run_bass_kernel_spmd

F32 = mybir.dt.float32
BF16 = mybir.dt.bfloat16
AF = mybir.ActivationFunctionType
ALU = mybir.AluOpType
AX = mybir.AxisListType

D = 1024
DFF = 2816
NF = 22
NH = 16
DH = 64
S_PROMPT = 4096
PAST = 1024
SS = 64
NCORES = 8
R_RING = 5
DEBUG_MEMSET = False
EPS = 1e-6
MASKV = -240000.0
POOL_W = (2, 4, 8, 16)

NBAND = 20


def _band_consts():
    b = np.zeros((NBAND, 128, 128), np.float32)
    s = np.arange(128)[:, None]
    t = np.arange(128)[None, :]
    for g, w in enumerate(POOL_W):
        inwin = (s <= t) & (s > t - w)
        b[g] = inwin / w - (s == t)
        b[4 + g] = ((s - 128) > (t - w)) / w
        cnt = np.minimum(t + 1, w)
        b[8 + g] = inwin / cnt - (s == t)
        same = (s // 64) == (t // 64)
        b[12 + g] = (inwin & same) / w - (s == t)
        hs = np.zeros((128, 128), np.float32)
        for q in range(2):
            for i in range(15):
                for tl in range(64):
                    if i > 15 + tl - w:
                        hs[16 * q + i, 64 * q + tl] = 1.0 / w
        b[16 + g] = hs
    return b


def _consts():
    s = np.arange(128)[:, None]
    t = np.arange(128)[None, :]
    ident = np.eye(128, dtype=np.float32)
    mask = np.where(s > t, MASKV, 0.0).astype(np.float32)
    bands = _band_consts()
    cb = np.concatenate([ident[:, None, :], mask[:, None, :], bands.transpose(1, 0, 2)], axis=1)
    cb = cb.reshape(128, (2 + NBAND) * 128).astype(ml_dtypes.bfloat16)
    U = (s <= t).astype(np.float32)
    ONES = np.ones((128, 128), np.float32)
    same = ((s // 64) == (t // 64)).astype(np.float32)
    cf = np.concatenate([U, ONES, U * same, same], axis=1)
    return cb, cf


class Sched:
    def __init__(self):
        self.q = {e: [] for e in ("pe", "act", "dve", "pool", "sp")}
        self.cnt = {}
        self.waited = {e: {} for e in self.q}
        self.res = {}

    def _deps(self, reads, writes, eng):
        deps = {}

        def add(tok):
            if tok is None:
                return
            sname, v = tok
            if deps.get(sname, 0) < v:
                deps[sname] = v

        for r in reads:
            st = self.res.get(r)
            if st:
                add(st["w"])
        for w in writes:
            st = self.res.get(w)
            if st:
                add(st["w"])
                for sname, v in st["r"].items():
                    add((sname, v))
        if eng == "pe":
            deps.pop("e_pe", None)
        return deps

    def _mark(self, tok, reads, writes):
        sname, v = tok
        for r in reads:
            st = self.res.setdefault(r, {"w": None, "r": {}})
            if st["r"].get(sname, 0) < v:
                st["r"][sname] = v
        for w in writes:
            self.res[w] = {"w": tok, "r": {}}

    def _waits(self, eng, deps):
        for sname, v in deps.items():
            if self.waited[eng].get(sname, 0) >= v:
                continue
            self.waited[eng][sname] = v
            self.q[eng].append(("wait", sname, v))

    def op(self, eng, fn, reads=(), writes=()):
        deps = self._deps(reads, writes, eng)
        self._waits(eng, deps)
        sname = "e_" + eng
        self.cnt[sname] = self.cnt.get(sname, 0) + 1
        tok = (sname, self.cnt[sname])
        self.q[eng].append(("op", fn, sname, 1))
        self._mark(tok, reads, writes)
        return tok

    def dma(self, eng, sname, out, in_, reads=(), writes=(), drain=False):
        deps = self._deps(reads, writes, eng)
        self._waits(eng, deps)
        self.cnt[sname] = self.cnt.get(sname, 0) + 16
        tok = (sname, self.cnt[sname])
        self.q[eng].append(("op", (lambda e, o=out, i=in_: e.dma_start(out=o, in_=i)), sname, 16))
        self._mark(tok, reads, writes)
        if drain:
            self._waits(eng, {sname: tok[1]})
        return tok

    def wait_tok(self, eng, tok):
        self._waits(eng, {tok[0]: tok[1]})


def build_program(S=S_PROMPT, past=PAST, ss_len=SS):
    nc = bass.Bass("TRN2", target_bir_lowering=False)
    sch = Sched()
    assert S % 512 == 0 and past % 512 == 0 and ss_len == 64
    KMAX = max(S, past + 128)

    def din(name, shape, dt=F32):
        return nc.dram_tensor(name, list(shape), dt, kind="ExternalInput").ap()

    def dout(name, shape, dt=F32):
        return nc.dram_tensor(name, list(shape), dt, kind="ExternalOutput").ap()

    def dscr(name, shape, dt=BF16):
        return nc.dram_tensor(name, list(shape), dt, kind="Internal").ap()

    xp = din("xp", [2, S, D]); xs = din("xs", [2, ss_len, D]); spl = din("spl", [2, 15, D])
    ck = din("ck", [2, past, D]); cv = din("cv", [2, past, D]); cl = din("cl", [2, past, NH])
    pool_w = din("pool_w", [D, 256]); w_k = din("w_k", [D, D]); w_v = din("w_v", [D, D])
    w_f = din("w_f", [D, NH]); w_q = din("w_q", [D, D]); w_o = din("w_o", [D, D])
    w_gate = din("w_gate", [2, D, DFF]); w_up = din("w_up", [2, D, DFF]); w_down = din("w_down", [2, DFF, D])
    cbf_d = din("cbf", [128, (2 + NBAND) * 128], BF16); cf_d = din("cf32", [128, 512])
    gbc_d = din("gbc", [128, 2 * D + 64 + 64 + 16 + 32])

    y_p = dout("y_p", [2, S, D]); y_s = dout("y_s", [2, ss_len, D])
    ps_p = dout("ps_p", [2, 15, D]); ps_s = dout("ps_s", [2, 15, D])
    k_p = dout("k_p", [2, S, D]); v_p = dout("v_p", [2, S, D]); lf_p = dout("lf_p", [2, S, NH])
    k_s = dout("k_s", [2, ss_len, D]); v_s = dout("v_s", [2, ss_len, D]); lf_s = dout("lf_s", [2, ss_len, NH])

    wb_pool = dscr("wb_pool", [D, 256]); wb_k = dscr("wb_k", [D, D]); wb_v = dscr("wb_v", [D, D])
    wb_q = dscr("wb_q", [D, D]); wb_o = dscr("wb_o", [D, D])
    wb_gate = dscr("wb_gate", [2, D, DFF]); wb_up = dscr("wb_up", [2, D, DFF]); wb_down = dscr("wb_down", [2, DFF, D])
    KTs = dscr("KTs", [4, NH, 68, KMAX]); Vs = dscr("Vs", [4, 8, KMAX, 192])

    es = contextlib.ExitStack()

    def sb(name, shape, dt):
        return es.enter_context(nc.sbuf_tensor(name, list(shape), dt))

    def ps(name, shape, dt):
        return es.enter_context(nc.psum_tensor(name, list(shape), dt))

    with es:
        x = sb("x", [128, 4, D], F32)
        h = sb("h", [128, 4, D], BF16)
        hprev = sb("hprev", [128, D], BF16)
        hTa = sb("hTa", [128, 8, 512], BF16)
        regA = sb("regA", [128, NF, 512], BF16)
        regB = sb("regB", [128, 2, 512], F32)
        ring = sb("ring", [128, R_RING, 4096], BF16)
        Fp = sb("Fp", [128, 3, D], F32)
        qk_aug = sb("qk_aug", [128, 4, NH, 68], BF16)
        pcA = sb("pcA", [128, 4, NH], F32); pcB = sb("pcB", [128, 4, NH], F32)
        Vaug = sb("Vaug", [128, 4, 8, 192], BF16)
        QT = sb("QT", [128, NH, 512], BF16)
        KT = sb("KT", [128, NH, 512], BF16)
        KTold = sb("KTold", [128, 3, 1024], BF16)
        Vold = sb("Vold", [128, 3, 8, 128], BF16)
        PT = sb("PT", [128, 4, 512], BF16)
        negD = sb("negD", [128, KMAX // 128 + 2, NH], F32)
        negDs = sb("negDs", [128, 2 * (past // 128), NH], F32)
        cbf = sb("cbf_sb", [128, 2 + NBAND, 128], BF16)
        cf = sb("cf_sb", [128, 4, 128], F32)
        gbc = sb("gbc_sb", [128, 2 * D + 64 + 64 + 16 + 32], F32)
        wf = sb("wf_sb", [128, 8, NH], BF16)
        spool = sb("spool", [32, D], BF16)
        lfc = sb("lfc", [128, past // 128, NH], F32)
        ssq = sb("ssq", [128, 4], F32); lnv = sb("lnv", [128, 4], F32); rstd = sb("rstd", [128, 4], F32)
        ssh = sb("ssh", [128, NH], F32); lnh = sb("lnh", [128, NH], F32); rsh = sb("rsh", [128, NH], F32)
        zb = sb("zb", [128, NH], F32); az = sb("az", [128, NH], F32); ez = sb("ez", [128, NH], F32)
        lz = sb("lz", [128, NH], F32); mz = sb("mz", [128, NH], F32)
        lf = sb("lf", [128, 4, NH], F32)
        Dt = sb("Dt", [128, 4, NH], F32)
        Dprev = sb("Dprev", [128, NH], F32)
        DprevS = sb("DprevS", [128, NH], F32)

        P = [ps("P%d" % i, [128, 1024], F32) for i in range(3)]
        T = [ps("T%d" % i, [128, 1024], BF16) for i in range(2)]

        ident = cbf[:, 0, :]
        masktri = cbf[:, 1, :]

        def band(i):
            return cbf[:, 2 + i, :]

        gmix0_bc = gbc[:, 0:D]
        pscale_bc = gbc[:, D:2 * D]
        gq_bc = gbc[:, 2 * D:2 * D + 64]
        gk_bc = gbc[:, 2 * D + 64:2 * D + 128]
        bf_bc = gbc[:, 2 * D + 128:2 * D + 144]
        gcols = gbc[:, 2 * D + 144:2 * D + 176]
        aT = regA
        hTb = regA[:, 0:8, :]
        OT = regA[:, 8:16, :]
        regBf = regB[:, :, :].rearrange("p a b -> p (a b)")

        def A(f, j):
            return ("A", f, j)

        ctok = []
        ctok.append(sch.dma("sp", "d_const", cbf[:, :, :].rearrange("p a b -> p (a b)"), cbf_d[:, :], writes=["c1"]))
        ctok.append(sch.dma("sp", "d_const", cf[:, :, :].rearrange("p a b -> p (a b)"), cf_d[:, :], writes=["c2"]))
        ctok.append(sch.dma("sp", "d_const", gbc[:, :], gbc_d[:, :], writes=["c3"]))
        for e in ("pe", "act", "dve", "pool"):
            sch.wait_tok(e, ctok[-1])

        prep_n = [0]

        def prep(dst, src, key):
            sch.dma("pool", "d_prep%d" % prep_n[0], dst, src, writes=[("wscr", key)], drain=True)
            prep_n[0] += 1

        prep(wb_pool[:, :], pool_w[:, :], "wp")
        GCH = [(0, 1024), (1024, 2048), (2048, DFF)]
        for l in range(2):
            for ci, (c0_, c1_) in enumerate(GCH):
                prep(wb_gate[l][:, c0_:c1_], w_gate[l][:, c0_:c1_], ("g", l, ci))
                prep(wb_up[l][:, c0_:c1_], w_up[l][:, c0_:c1_], ("u", l, ci))
            if l == 0:
                prep(wb_down[l], w_down[l], ("d", 0))
                prep(wb_k[:, :], w_k[:, :], "wk"); prep(wb_v[:, :], w_v[:, :], "wv")
                prep(wb_q[:, :], w_q[:, :], "wq"); prep(wb_o[:, :], w_o[:, :], "wo")
        prep(wb_down[1], w_down[1], ("d", 1))
        sch.dma("pool", "d_wf", wf[:, :, :], w_f.rearrange("(k p) c -> p k c", p=128), writes=["wf"])
        sch.op("dve", lambda e: e.memset(Vaug[:, :, :, :].rearrange("p a b c -> p (a b c)"), 1.0),
               writes=[("VA", j) for j in range(4)])
        sch.op("dve", lambda e: e.memset(spool[:, :], 0.0), writes=["spool"])
        if DEBUG_MEMSET:
            sch.op("dve", lambda e: e.memset(QT[:, :, :].rearrange("p a b -> p (a b)"), 0.0), writes=[])
            sch.op("dve", lambda e: e.memset(KT[:, :, :].rearrange("p a b -> p (a b)"), 0.0), writes=[])

        tile_blocks = ([("wp",)] + [("gu", 0, b) for b in range(11)]
                       + [("dn", 0, dh, fb) for dh in range(2) for fb in range(3)]
                       + [("wk", hf) for hf in range(2)] + [("wv", hf) for hf in range(2)]
                       + [("wq", hf) for hf in range(2)] + [("wo", hf) for hf in range(2)]
                       + [("gu", 1, b) for b in range(11)]
                       + [("dn", 1, dh, fb) for dh in range(2) for fb in range(3)])
        n_tiles = 2 * (S // 512) + 1
        blocks = tile_blocks * n_tiles
        ring_state = {"take": 0, "rel": 0, "issued": 0}

        def ring_issue():
            bi = ring_state["issued"]
            if bi >= len(blocks):
                return
            ring_state["issued"] += 1
            s = bi % R_RING
            b = blocks[bi]
            sem = "d_ring%d" % s
            if b[0] == "wp":
                rd = [("wscr", "wp")]
            elif b[0] == "gu":
                ci_ = min(b[2] // 4, 2)
                rd = [("wscr", ("g", b[1], ci_)), ("wscr", ("u", b[1], ci_))]
            elif b[0] == "dn":
                rd = [("wscr", ("d", b[1]))]
            else:
                rd = [("wscr", b[0])]
            wr = [("ring", s)]
            kp = "(k p) c -> p k c"
            if b[0] == "wp":
                sch.dma("sp", sem, ring[:, s, 0:2048].rearrange("p (k c) -> p k c", c=256),
                        wb_pool.rearrange(kp, p=128), rd, wr)
            elif b[0] == "gu":
                _, l, bb = b
                sch.dma("sp", sem, ring[:, s, 0:2048].rearrange("p (k c) -> p k c", c=256),
                        wb_gate[l][:, bb * 256:(bb + 1) * 256].rearrange(kp, p=128), rd, wr)
                sch.dma("sp", sem, ring[:, s, 2048:4096].rearrange("p (k c) -> p k c", c=256),
                        wb_up[l][:, bb * 256:(bb + 1) * 256].rearrange(kp, p=128), rd, wr)
            elif b[0] == "dn":
                _, l, dh, fb = b
                nf = 8 if fb < 2 else 6
                sch.dma("sp", sem, ring[:, s, 0:nf * 512].rearrange("p (f c) -> p f c", c=512),
                        wb_down[l][fb * 1024:fb * 1024 + nf * 128, dh * 512:(dh + 1) * 512].rearrange("(f p) c -> p f c", p=128),
                        rd, wr)
            else:
                src = {"wk": wb_k, "wv": wb_v, "wq": wb_q, "wo": wb_o}[b[0]]
                hf = b[1]
                sch.dma("sp", sem, ring[:, s, :].rearrange("p (k c) -> p k c", c=512),
                        src[:, hf * 512:(hf + 1) * 512].rearrange(kp, p=128), rd, wr)

        def ring_take(kind):
            bi = ring_state["take"]
            assert blocks[bi][0] == kind, (blocks[bi], kind)
            ring_state["take"] += 1
            return bi % R_RING

        def ring_release(n=1):
            for _ in range(n):
                ring_state["rel"] += 1
                ring_issue()

        rot = {"T": 0, "P": 0, "P2": 0, "F": 0, "bank": 0, "S": 0, "O": 0, "PT": 0, "KO": 0}

        def nxt(k, n):
            v = rot[k]
            rot[k] = (v + 1) % n
            return v

        def norm_j(j):
            sch.op("act", lambda e, j=j: e.activation(out=regBf, in_=x[:, j, :], func=AF.Square,
                                                      accum_out=ssq[:, j:j + 1]),
                   reads=[("x", j)], writes=[("B", 0), ("B", 1), ("ssq", j)])
            sch.op("act", lambda e, j=j: e.activation(out=lnv[:, j:j + 1], in_=ssq[:, j:j + 1], func=AF.Ln,
                                                      scale=1.0 / D, bias=eps_t[:, 0:1]),
                   reads=[("ssq", j)], writes=[("lnv", j)])
            sch.op("act", lambda e, j=j: e.activation(out=rstd[:, j:j + 1], in_=lnv[:, j:j + 1], func=AF.Exp, scale=-0.5),
                   reads=[("lnv", j)], writes=[("rstd", j)])

        def scale_h(tl, j):
            sch.op("act", lambda e, j=j: e.activation(out=h[:, j, :], in_=x[:, j, :], func=AF.Copy,
                                                      scale=rstd[:, j:j + 1]),
                   reads=[("x", j), ("rstd", j)], writes=[("h", j)])

        def transposes(j):
            tb = nxt("T", 2)

            def fn(e, j=j, tb=tb):
                ins = None
                for k in range(8):
                    ins = e.transpose(out=T[tb][:, k * 128:(k + 1) * 128], in_=h[:, j, k * 128:(k + 1) * 128],
                                      identity=ident)
                return ins
            sch.op("pe", fn, reads=[("h", j)], writes=[("T", tb)])
            return tb

        def evac_gain(tb, j, dst, dkeys, gi):
            sch.op("dve", lambda e, j=j, tb=tb: e.tensor_tensor(
                out=dst[:, :, j * 128:(j + 1) * 128], in0=T[tb][:, :].rearrange("p (k t) -> p k t", t=128),
                in1=gcols[:, gi * 8:(gi + 1) * 8].unsqueeze(2).to_broadcast([128, 8, 128]), op=ALU.mult),
                reads=[("T", tb)], writes=dkeys)

        def ffn(tl, l):
            NT = tl["NT"]; ntok = NT * 128
            for j in range(NT):
                norm_j(j)
                scale_h(tl, j)
            for j in range(NT):
                tb = transposes(j)
                evac_gain(tb, j, hTa, [("hTa", j)], 0 if l == 0 else 3)
            for b in range(11):
                s = ring_take("gu")
                for fl in range(2):
                    f = 2 * b + fl
                    n = nxt("P", 3)

                    def fng(e, s=s, fl=fl, n=n):
                        ins = None
                        for k in range(8):
                            ins = e.matmul(P[n][:, 0:ntok], lhsT=ring[:, s, k * 256 + fl * 128:k * 256 + fl * 128 + 128],
                                           rhs=hTa[:, k, 0:ntok], start=(k == 0), stop=(k == 7))
                        return ins

                    def fnu(e, s=s, fl=fl, n=n):
                        ins = None
                        for k in range(8):
                            ins = e.matmul(P[n][:, 512:512 + ntok],
                                           lhsT=ring[:, s, 2048 + k * 256 + fl * 128:2048 + k * 256 + fl * 128 + 128],
                                           rhs=hTa[:, k, 0:ntok], start=(k == 0), stop=(k == 7))
                        return ins
                    hk = [("hTa", j) for j in range(NT)]
                    sch.op("pe", fng, reads=[("ring", s)] + hk, writes=[("P", n, 0)])
                    sch.op("pe", fnu, reads=[("ring", s)] + hk, writes=[("P", n, 1)])
                    r = f % 2
                    sch.op("act", lambda e, n=n, r=r: e.activation(out=regB[:, r, 0:ntok], in_=P[n][:, 0:ntok], func=AF.Silu),
                           reads=[("P", n, 0)], writes=[("B", r)])
                    sch.op("dve", lambda e, n=n, r=r, f=f: e.tensor_tensor(out=aT[:, f, 0:ntok], in0=regB[:, r, 0:ntok],
                                                                         in1=P[n][:, 512:512 + ntok], op=ALU.mult),
                           reads=[("B", r), ("P", n, 1)], writes=[A(f, j) for j in range(NT)])
                ring_release()
            for dh in range(2):
                slots = [ring_take("dn") for _ in range(3)]
                for j in range(NT):
                    n = nxt("bank", 6)
                    pn, half = n // 2, n % 2

                    def fnd(e, j=j, pn=pn, half=half, slots=slots):
                        ins = None
                        for f in range(NF):
                            s = slots[f // 8]
                            fo = (f % 8) * 512
                            ins = e.matmul(P[pn][:, half * 512:half * 512 + 512], lhsT=aT[:, f, j * 128:(j + 1) * 128],
                                           rhs=ring[:, s, fo:fo + 512], start=(f == 0), stop=(f == NF - 1))
                        return ins
                    sch.op("pe", fnd, reads=[("ring", s) for s in slots] + [A(f, j) for f in range(NF)],
                           writes=[("P", pn, half)])
                    sch.op("dve", lambda e, j=j, pn=pn, half=half, dh=dh: e.tensor_tensor(
                        out=x[:, j, dh * 512:(dh + 1) * 512], in0=x[:, j, dh * 512:(dh + 1) * 512],
                        in1=P[pn][:, half * 512:half * 512 + 512], op=ALU.add),
                        reads=[("P", pn, half), ("x", j)], writes=[("x", j)])
                    if l == 1 and dh == 1:
                        store_y(tl, j)
                ring_release(3)

        def store_y(tl, j):
            if tl["kind"] == "p":
                dst = y_p[tl["seq"], tl["t0"] + j * 128:tl["t0"] + (j + 1) * 128, :]
            else:
                dst = y_s.rearrange("b s d -> (b s) d")
            sch.dma("pool", "d_y%d" % j, dst, x[:, j, :], reads=[("x", j)], writes=[])
            nt = tl.get("next")
            if nt is not None and j < nt["NT"]:
                load_x(nt, j)

        def load_x(tl, j):
            if tl["kind"] == "p":
                src = xp[tl["seq"], tl["t0"] + j * 128:tl["t0"] + (j + 1) * 128, :]
            else:
                src = xs.rearrange("b s d -> (b s) d")
            sch.dma("sp", "d_x%d" % j, x[:, j, :], src, writes=[("x", j)])

        def pool_layer(tl):
            NT = tl["NT"]
            smp = tl["kind"] == "s"
            for j in range(NT):
                norm_j(j)
                sch.op("dve", lambda e, j=j: e.scalar_tensor_tensor(out=h[:, j, :], in0=x[:, j, :], scalar=rstd[:, j:j + 1],
                                                                    in1=gmix0_bc, op0=ALU.mult, op1=ALU.mult),
                       reads=[("x", j), ("rstd", j)], writes=[("h", j)])
                if tl["last"] and j == NT - 1:
                    fi = nxt("F", 3)
                    sch.op("dve", lambda e, j=j, fi=fi: e.scalar_tensor_tensor(out=Fp[:, fi, :], in0=x[:, j, :],
                                                                               scalar=rstd[:, j:j + 1], in1=gmix0_bc,
                                                                               op0=ALU.mult, op1=ALU.mult),
                           reads=[("x", j), ("rstd", j)], writes=[("F", fi)])
                    if smp:
                        sch.dma("pool", "d_F%d" % fi, ps_s[0, :, :], Fp[49:64, fi, :], reads=[("F", fi)])
                        sch.dma("pool", "d_F%d" % fi, ps_s[1, :, :], Fp[113:128, fi, :], reads=[("F", fi)])
                    else:
                        sch.dma("pool", "d_F%d" % fi, ps_p[tl["seq"], :, :], Fp[113:128, fi, :], reads=[("F", fi)])
            sp_ = ring_take("wp")
            for j in range(NT):
                n = nxt("P", 3)
                first = tl["first"] and j == 0

                def fnp(e, j=j, n=n, first=first):
                    ins = None
                    for c in range(8):
                        g = c // 2
                        o = P[n][:, c * 128:(c + 1) * 128]
                        lt = h[:, j, c * 128:(c + 1) * 128]
                        if smp:
                            e.matmul(o, lhsT=lt, rhs=band(12 + g), start=True, stop=False)
                            ins = e.matmul(o, lhsT=spool[0:32, c * 128:(c + 1) * 128], rhs=cbf[0:32, 2 + 16 + g, :],
                                           start=False, stop=True)
                        elif first:
                            ins = e.matmul(o, lhsT=lt, rhs=band(8 + g), start=True, stop=True)
                        else:
                            e.matmul(o, lhsT=lt, rhs=band(g), start=True, stop=False)
                            hp = hprev[:, c * 128:(c + 1) * 128] if j == 0 else h[:, j - 1, c * 128:(c + 1) * 128]
                            ins = e.matmul(o, lhsT=hp, rhs=band(4 + g), start=False, stop=True)
                    return ins
                rds = [("h", j)]
                if smp:
                    rds.append("spool")
                elif not first:
                    rds.append("hprev" if j == 0 else ("h", j - 1))
                sch.op("pe", fnp, reads=rds, writes=[("P", n, 0), ("P", n, 1)])
                sch.op("act", lambda e, j=j, n=n: e.activation(out=hTa[:, :, j * 128:(j + 1) * 128],
                                                               in_=P[n][:, :].rearrange("p (c t) -> p c t", t=128), func=AF.Copy),
                       reads=[("P", n, 0), ("P", n, 1)], writes=[("hTa", j)])
            if not smp and not tl["last"]:
                sch.op("dve", lambda e: e.tensor_copy(out=hprev[:, :], in_=h[:, NT - 1, :]),
                       reads=[("h", NT - 1)], writes=["hprev"])
            for j in range(NT):
                n = nxt("P", 3)

                def fnw(e, j=j, n=n):
                    ins = None
                    for g in range(4):
                        for cc in range(2):
                            c = 2 * g + cc
                            ins = e.matmul(P[n][:, g * 256:(g + 1) * 256], lhsT=hTa[:, c, j * 128:(j + 1) * 128],
                                           rhs=ring[:, sp_, c * 256:(c + 1) * 256], start=(cc == 0), stop=(cc == 1))
                    return ins
                sch.op("pe", fnw, reads=[("hTa", j), ("ring", sp_)], writes=[("P", n, 0), ("P", n, 1)])
                fi = nxt("F", 3)
                sch.op("dve", lambda e, n=n, fi=fi: e.tensor_tensor(out=Fp[:, fi, :], in0=P[n][:, :], in1=pscale_bc, op=ALU.mult),
                       reads=[("P", n, 0), ("P", n, 1)], writes=[("F", fi)])
                sch.op("dve", lambda e, j=j, fi=fi: e.tensor_tensor(out=x[:, j, :], in0=x[:, j, :], in1=Fp[:, fi, :], op=ALU.add),
                       reads=[("F", fi), ("x", j)], writes=[("x", j)])
            ring_release()

        def qk_transposes(jl, dst, dname):
            for j in jl:
                for hg in range(2):
                    tb = nxt("T", 2)

                    def fnt(e, j=j, hg=hg, tb=tb):
                        ins = None
                        for hh in range(8):
                            ins = e.transpose(out=T[tb][0:68, hh * 128:(hh + 1) * 128], in_=qk_aug[:, j, hg * 8 + hh, 0:68],
                                              identity=ident)
                        return ins
                    sch.op("pe", fnt, reads=[("QK", j)], writes=[("T", tb)])
                    if hg == 0:
                        sch.op("dve", lambda e, j=j, hg=hg, tb=tb: e.tensor_copy(
                            out=dst[0:68, hg * 8:(hg + 1) * 8, j * 128:(j + 1) * 128],
                            in_=T[tb][0:68, :].rearrange("p (a t) -> p a t", t=128)),
                            reads=[("T", tb)], writes=[(dname, hg, j)])
                    else:
                        sch.op("act", lambda e, j=j, hg=hg, tb=tb: e.activation(
                            out=dst[0:68, hg * 8:(hg + 1) * 8, j * 128:(j + 1) * 128],
                            in_=T[tb][0:68, :].rearrange("p (a t) -> p a t", t=128), func=AF.Copy),
                            reads=[("T", tb)], writes=[(dname, hg, j)])

        def headnorm(n, gbcast, out_ap, out_keys, j, extra_reads=()):
            sch.op("act", lambda e, n=n: e.activation(out=regBf, in_=P[n][:, :], func=AF.Square),
                   reads=[("P", n, 0), ("P", n, 1)], writes=[("B", 0), ("B", 1)])
            sch.op("dve", lambda e: e.tensor_reduce(out=ssh[:, :], in_=regBf.rearrange("p (a d) -> p a d", d=64),
                                                    axis=AX.X, op=ALU.add),
                   reads=[("B", 0), ("B", 1)], writes=["ssh"])
            sch.op("act", lambda e: e.activation(out=lnh[:, :], in_=ssh[:, :], func=AF.Ln, scale=1.0 / DH, bias=eps_t[:, 0:1]),
                   reads=["ssh"], writes=["lnh"])
            sch.op("act", lambda e: e.activation(out=rsh[:, :], in_=lnh[:, :], func=AF.Exp, scale=-0.5),
                   reads=["lnh"], writes=["rsh"])
            pv_ = P[n][:, :].rearrange("p (a d) -> p a d", d=64)
            rb = rsh[:, :].unsqueeze(2).to_broadcast([128, NH, DH])
            if out_ap is not None:
                sch.op("dve", lambda e: e.tensor_tensor(out=out_ap, in0=pv_, in1=rb, op=ALU.mult),
                       reads=[("P", n, 0), ("P", n, 1), "rsh"], writes=out_keys)
                return None
            fi = nxt("F", 3)
            fv = Fp[:, fi, :].rearrange("p (a d) -> p a d", d=64)
            sch.op("dve", lambda e: e.tensor_tensor(out=fv, in0=pv_, in1=rb, op=ALU.mult),
                   reads=[("P", n, 0), ("P", n, 1), "rsh"], writes=[("F", fi)])
            sch.op("dve", lambda e: e.tensor_tensor(out=fv, in0=fv, in1=gbcast.unsqueeze(1).to_broadcast([128, NH, DH]), op=ALU.mult),
                   reads=[("F", fi)], writes=[("F", fi)])
            return fi

        def proj(tl, j, src, skeys, kind):
            n = nxt("P2", 2)
            for hf in range(2):
                s = tl["slots"][kind][hf]

                def fn(e, s=s, hf=hf, n=n, j=j):
                    ins = None
                    for k in range(8):
                        ins = e.matmul(P[n][:, hf * 512:(hf + 1) * 512], lhsT=src[:, k, j * 128:(j + 1) * 128],
                                       rhs=ring[:, s, k * 512:(k + 1) * 512], start=(k == 0), stop=(k == 7))
                    return ins
                sch.op("pe", fn, reads=[("ring", s)] + skeys(j), writes=[("P", n, hf)])
            return n

        def tok_rows(tl, j, dram_p, dram_s):
            if tl["kind"] == "p":
                return dram_p[tl["seq"], tl["t0"] + j * 128:tl["t0"] + (j + 1) * 128, :]
            return dram_s.rearrange("b s d -> (b s) d")

        def kvq(tl):
            NT = tl["NT"]
            smp = tl["kind"] == "s"
            kc0 = tl["t0"] // 128
            for j in range(NT):
                norm_j(j)
                scale_h(tl, j)
            for j in range(NT):
                tb = transposes(j)
                evac_gain(tb, j, hTb, [A(k, j) for k in range(8)], 1)
                evac_gain(tb, j, hTa, [("hTa", j)], 2)
            tl["slots"] = {}
            hb = lambda j: [A(k, j) for k in range(8)]
            ha = lambda j: [("hTa", j)]
            for j in range(NT):
                def fnz(e, j=j):
                    ins = None
                    for k in range(8):
                        ins = e.matmul(P[2][:, j * NH:(j + 1) * NH], lhsT=hTb[:, k, j * 128:(j + 1) * 128], rhs=wf[:, k, :],
                                       start=(k == 0), stop=(k == 7))
                    return ins
                sch.op("pe", fnz, reads=hb(j) + ["wf"], writes=[("P", 2, 0)])
            for j in range(NT):
                sch.op("dve", lambda e, j=j: e.tensor_tensor(out=zb[:, :], in0=P[2][:, j * NH:(j + 1) * NH], in1=bf_bc, op=ALU.add),
                       reads=[("P", 2, 0)], writes=["zb"])
                sch.op("act", lambda e: e.activation(out=az[:, :], in_=zb[:, :], func=AF.Abs),
                       reads=["zb"], writes=["az"])
                sch.op("act", lambda e: e.activation(out=ez[:, :], in_=az[:, :], func=AF.Exp, scale=-1.0),
                       reads=["az"], writes=["ez"])
                sch.op("act", lambda e: e.activation(out=lz[:, :], in_=ez[:, :], func=AF.Ln, bias=one_t[:, 0:1]),
                       reads=["ez"], writes=["lz"])
                sch.op("dve", lambda e: e.tensor_scalar(out=mz[:, :], in0=zb[:, :], scalar1=0.0, scalar2=None, op0=ALU.min),
                       reads=["zb"], writes=["mz"])
                sch.op("dve", lambda e, j=j: e.tensor_tensor(out=lf[:, j, :], in0=mz[:, :], in1=lz[:, :], op=ALU.subtract),
                       reads=["mz", "lz"], writes=[("lf", j)])
                if smp:
                    ldst = lf_s.rearrange("b s c -> (b s) c")
                else:
                    ldst = lf_p[tl["seq"], tl["t0"] + j * 128:tl["t0"] + (j + 1) * 128, :]
                sch.dma("pool", "d_lf%d" % j, ldst, lf[:, j, :], reads=[("lf", j)])
            tl["slots"]["wk"] = [ring_take("wk") for _ in range(2)]
            sch.op("dve", lambda e: e.memset(qk_aug[:, :, :, 64], 1.0), writes=[("QK", j) for j in range(4)])
            for j in range(NT):
                n = proj(tl, j, hTb, hb, "wk")
                fi = headnorm(n, gk_bc, None, None, j)
                sch.dma("pool", "d_F%d" % fi, tok_rows(tl, j, k_p, k_s), Fp[:, fi, :], reads=[("F", fi)])
                sch.op("dve", lambda e, j=j, fi=fi: e.tensor_tensor(out=qk_aug[:, j, :, 0:64],
                                                                     in0=Fp[:, fi, :].rearrange("p (a d) -> p a d", d=64),
                                                                     in1=gq_bc.unsqueeze(1).to_broadcast([128, NH, DH]), op=ALU.mult),
                       reads=[("F", fi)], writes=[("QK", j)])
            ring_release(2)
            for j in range(NT):
                cum_pe(lf[:, j, :], [("lf", j)], smp, j)
            for j in range(NT):
                cum_dve(DprevS if smp else Dprev, "DprevS" if smp else "Dprev",
                        negD[:, kc0 + j, :], [("ND", kc0 + j)], Dt[:, j, :], [("Dt", j)], j)
            nd_pieces(negD[:, kc0:kc0 + NT, :], [("ND", kc0 + j) for j in range(NT)], NT)
            tl["slots"]["wv"] = [ring_take("wv") for _ in range(2)]
            for j in range(NT):
                n = proj(tl, j, hTb, hb, "wv")
                fi = nxt("F", 3)
                sch.op("act", lambda e, n=n, fi=fi: e.activation(out=Fp[:, fi, :], in_=P[n][:, :], func=AF.Copy),
                       reads=[("P", n, 0), ("P", n, 1)], writes=[("F", fi)])
                sch.dma("pool", "d_F%d" % fi, tok_rows(tl, j, v_p, v_s), Fp[:, fi, :], reads=[("F", fi)])

                def fnv(e, j=j, fi=fi):
                    fv = Fp[:, fi, :].rearrange("p (a b d) -> p a b d", b=2, d=64)
                    e.activation(out=Vaug[:, j, :, 0:64], in_=fv[:, :, 0, :], func=AF.Copy)
                    return e.activation(out=Vaug[:, j, :, 128:192], in_=fv[:, :, 1, :], func=AF.Copy)
                sch.op("act", fnv, reads=[("F", fi)], writes=[("VA", j)])
            store_v_tile(tl)
            ring_release(2)
            qk_transposes(range(NT), KT, "KT")
            store_kt(tl, NT * 128)
            sch.op("dve", lambda e: e.memset(qk_aug[:, :, :, 65:68], 1.0), writes=[("QK", j) for j in range(4)])
            tl["slots"]["wq"] = [ring_take("wq") for _ in range(2)]
            for j in range(NT):
                n = proj(tl, j, hTa, ha, "wq")
                headnorm(n, gq_bc, qk_aug[:, j, :, 0:64], [("QK", j)], j)
                sch.op("dve", lambda e, j=j: e.tensor_scalar(out=qk_aug[:, j, :, 64], in0=Dt[:, j, :], scalar1=8.0,
                                                             scalar2=None, op0=ALU.mult),
                       reads=[("Dt", j)], writes=[("QK", j)])
            ring_release(2)
            qk_transposes(range(NT), QT, "QT")

        def nd_pieces(nd_src, nd_keys, n):
            qkk = [("QK", j) for j in range(n)]
            A_ = pcA[:, 0:n, :]
            B_ = pcB[:, 0:n, :]
            c65, c66, c67 = qk_aug[:, 0:n, :, 65], qk_aug[:, 0:n, :, 66], qk_aug[:, 0:n, :, 67]
            sch.op("dve", lambda e: e.tensor_scalar(out=A_, in0=nd_src, scalar1=8.0, scalar2=None, op0=ALU.mult),
                   reads=nd_keys, writes=["pcA"])
            sch.op("dve", lambda e: e.tensor_copy(out=c65, in_=A_), reads=["pcA"], writes=qkk)
            sch.op("dve", lambda e: e.tensor_tensor(out=B_, in0=A_, in1=c65, op=ALU.subtract), reads=["pcA"] + qkk, writes=["pcB"])
            sch.op("dve", lambda e: e.tensor_copy(out=c66, in_=B_), reads=["pcB"], writes=qkk)
            sch.op("dve", lambda e: e.tensor_tensor(out=A_, in0=B_, in1=c66, op=ALU.subtract), reads=["pcB"] + qkk, writes=["pcA"])
            sch.op("dve", lambda e: e.tensor_copy(out=c67, in_=A_), reads=["pcA"], writes=qkk)

        def cum_pe(lsrc, lkeys, smp, co):
            ui = 2 if smp else 0
            c0 = 512 + co * 2 * NH

            def fnc(e):
                e.matmul(P[2][:, c0:c0 + NH], lhsT=cf[:, ui, :], rhs=lsrc, start=True, stop=True)
                return e.matmul(P[2][:, c0 + NH:c0 + 2 * NH], lhsT=cf[:, ui + 1, :], rhs=lsrc, start=True, stop=True)
            sch.op("pe", fnc, reads=lkeys, writes=[("P", 2, 1)])

        def cum_dve(dprev, dpkey, nd_out, nd_keys, dt_out, dt_keys, co):
            c0 = 512 + co * 2 * NH
            rk = [("P", 2, 1), dpkey]
            if dt_out is not None:
                sch.op("dve", lambda e: e.tensor_tensor(out=dt_out, in0=P[2][:, c0:c0 + NH], in1=dprev[:, :], op=ALU.add),
                       reads=rk, writes=dt_keys)
            sch.op("dve", lambda e: e.scalar_tensor_tensor(out=nd_out, in0=P[2][:, c0:c0 + NH], scalar=-1.0, in1=dprev[:, :],
                                                           op0=ALU.mult, op1=ALU.subtract),
                   reads=rk, writes=nd_keys)
            sch.op("dve", lambda e: e.tensor_tensor(out=dprev[:, :], in0=dprev[:, :], in1=P[2][:, c0 + NH:c0 + 2 * NH], op=ALU.add),
                   reads=rk, writes=[dpkey])

        ing_keys = []

        def store_kt(tl, nk, slot=None, k0=None, eng="pool"):
            if tl is not None and tl["kind"] == "s":
                return
            if slot is None:
                slot, k0 = tl["seq"], tl["t0"]
                sem = "d_kt%d" % ((k0 // 512) % 2)
            else:
                sem = "d_kti"
                ing_keys.append(("KTs", slot, k0 // 512))
            return sch.dma(eng, sem, KTs[slot, :, :, k0:k0 + nk].rearrange("h r t -> r h t"), KT[0:68, :, 0:nk],
                           reads=[("KT", hg, j) for hg in range(2) for j in range(nk // 128)],
                           writes=[("KTs", slot, k0 // 512)])

        def store_v_tile(tl, slot=None, k0=None, eng="pool"):
            if tl is not None and tl["kind"] == "s":
                return
            if slot is None:
                slot, k0 = tl["seq"], tl["t0"]
                sem = "d_vs%d" % ((k0 // 512) % 2)
            else:
                sem = "d_vsi"
            tok = None
            for j in range(4):
                tok = sch.dma(eng, sem + "_%d" % j, Vs[slot, :, k0 + j * 128:k0 + (j + 1) * 128, :].rearrange("a t c -> t a c"),
                              Vaug[:, j, :, :], reads=[("VA", j)], writes=[("Vs", slot, k0 // 128 + j)])
            return tok

        def attend(tl):
            NT = tl["NT"]
            smp = tl["kind"] == "s"
            ntok = NT * 128
            items = []
            loads = []
            LA = 2

            def S_bank():
                sb_ = nxt("S", 3)
                return P[sb_ // 2][:, (sb_ % 2) * 512:(sb_ % 2) * 512 + 512], ("P", sb_ // 2, sb_ % 2)

            def mk_load(slot, hd, blk, nkc, lidx):
                pair, voff = hd // 2, (0 if hd % 2 == 0 else 64)

                def emit():
                    ko = lidx % 3
                    sch.dma("sp", "d_ko%d" % ko, KTold[0:68, ko, 0:nkc * 128],
                            KTs[slot, hd, :, blk * 1024:blk * 1024 + nkc * 128],
                            reads=[("KTs", slot, blk * 2), ("KTs", slot, blk * 2 + 1)], writes=[("KO", ko)])
                    sch.dma("sp", "d_vo%d" % ko, Vold[:, ko, 0:nkc, :],
                            Vs[slot, pair, blk * 1024:blk * 1024 + nkc * 128, voff:voff + 128].rearrange("(a p) c -> p a c", p=128),
                            reads=[("Vs", slot, blk * 8 + a_) for a_ in range(nkc)], writes=[("VO", ko)])
                return emit

            def mk_old(hd, psO, okey, lidx, a_, q0, nq, nd, ndk, st, sp_, pre):
                ko = lidx % 3
                it = {"pre": pre}

                def qk(it):
                    it["psS"], it["skey"] = S_bank()
                    sch.op("pe", lambda e, psS=it["psS"]: e.matmul(
                        psS[:, q0:q0 + nq], lhsT=KTold[0:68, ko, a_ * 128:(a_ + 1) * 128], rhs=QT[0:68, hd, q0:q0 + nq],
                        start=True, stop=True),
                        reads=[("KO", ko)] + [("QT", hd // 8, j) for j in range(NT)], writes=[it["skey"]])

                def ex(it):
                    it["pr"] = nxt("PT", 4)
                    sch.op("act", lambda e, psS=it["psS"], pr=it["pr"]: e.activation(
                        out=PT[:, pr, q0:q0 + nq], in_=psS[:, q0:q0 + nq], func=AF.Exp, scale=0.125),
                        reads=[it["skey"]] + ndk, writes=[("PT", it["pr"])])

                def pv(it):
                    sch.op("pe", lambda e, pr=it["pr"]: e.matmul(
                        psO[:, q0:q0 + nq], lhsT=Vold[:, ko, a_, :], rhs=PT[:, pr, q0:q0 + nq], start=st, stop=sp_),
                        reads=[("VO", ko), ("PT", it["pr"])], writes=[okey])
                it.update(qk=qk, ex=ex, pv=pv)
                return it

            def mk_diag(hd, psO, okey, dl, nq, st, sp_):
                pair, voff = hd // 2, (0 if hd % 2 == 0 else 64)
                kcg = tl["t0"] // 128 + dl
                c0 = dl * 128
                nqq = nq - c0
                it = {"pre": []}

                def qk(it):
                    it["psS"], it["skey"] = S_bank()

                    def fnq(e, psS=it["psS"]):
                        e.matmul(psS[:, c0:c0 + nqq], lhsT=KT[0:68, hd, dl * 128:(dl + 1) * 128],
                                 rhs=QT[0:68, hd, c0:c0 + nqq], start=True, stop=False)
                        return e.matmul(psS[:, c0:c0 + 128], lhsT=ident, rhs=masktri, start=False, stop=True)
                    sch.op("pe", fnq, reads=[("KT", hd // 8, dl)] + [("QT", hd // 8, j) for j in range(NT)], writes=[it["skey"]])

                def ex(it):
                    it["pr"] = nxt("PT", 4)
                    sch.op("act", lambda e, psS=it["psS"], pr=it["pr"]: e.activation(
                        out=PT[:, pr, c0:c0 + nqq], in_=psS[:, c0:c0 + nqq], func=AF.Exp, scale=0.125),
                        reads=[it["skey"], ("ND", kcg)], writes=[("PT", it["pr"])])

                def pv(it):
                    sch.op("pe", lambda e, pr=it["pr"]: e.matmul(
                        psO[:, c0:c0 + nqq], lhsT=Vaug[:, dl, pair, voff:voff + 128], rhs=PT[:, pr, c0:c0 + nqq],
                        start=st, stop=sp_),
                        reads=[("VA", dl), ("PT", it["pr"])], writes=[okey])
                it.update(qk=qk, ex=ex, pv=pv)
                return it

            def mk_diag_s(hd, psO, okey, q, st, sp_):
                pair, voff = hd // 2, (0 if hd % 2 == 0 else 64)
                pb = q * 64
                kcs = past // 128
                it = {"pre": []}

                def qk(it):
                    it["psS"], it["skey"] = S_bank()

                    def fnq(e, psS=it["psS"]):
                        e.matmul(psS[pb:pb + 64, pb:pb + 64], lhsT=KT[0:68, hd, pb:pb + 64],
                                 rhs=QT[0:68, hd, pb:pb + 64], start=True, stop=False)
                        return e.matmul(psS[pb:pb + 64, pb:pb + 64], lhsT=cbf[pb:pb + 64, 0, pb:pb + 64],
                                        rhs=cbf[pb:pb + 64, 1, pb:pb + 64], start=False, stop=True)
                    sch.op("pe", fnq, reads=[("KT", hd // 8, 0), ("QT", hd // 8, 0)], writes=[it["skey"]])

                def ex(it):
                    it["pr"] = nxt("PT", 4)
                    sch.op("act", lambda e, psS=it["psS"], pr=it["pr"]: e.activation(
                        out=PT[pb:pb + 64, pr, pb:pb + 64], in_=psS[pb:pb + 64, pb:pb + 64], func=AF.Exp, scale=0.125),
                        reads=[it["skey"], ("ND", kcs)], writes=[("PT", it["pr"])])

                def pv(it):
                    sch.op("pe", lambda e, pr=it["pr"]: e.matmul(
                        psO[:, pb:pb + 64], lhsT=Vaug[pb:pb + 64, 0, pair, voff:voff + 128],
                        rhs=PT[pb:pb + 64, pr, pb:pb + 64], start=st, stop=sp_),
                        reads=[("VA", 0), ("PT", it["pr"])], writes=[okey])
                it.update(qk=qk, ex=ex, pv=pv)
                return it

            def mk_norm(hd, psO, okey):
                pair, par = hd // 2, hd % 2
                o0, d0 = (0, 64) if par == 0 else (64, 0)

                def emit():
                    fi = nxt("F", 3)
                    sch.op("dve", lambda e: e.reciprocal(out=Fp[o0:o0 + 64, fi, 0:ntok], in_=psO[d0:d0 + 64, 0:ntok]),
                           reads=[okey], writes=[("F", fi)])
                    sch.op("dve", lambda e: e.tensor_tensor(out=OT[o0:o0 + 64, pair, 0:ntok], in0=psO[o0:o0 + 64, 0:ntok],
                                                           in1=Fp[o0:o0 + 64, fi, 0:ntok], op=ALU.mult),
                           reads=[okey, ("F", fi)], writes=[A(8 + pair, j) for j in range(NT)])
                return emit

            def ensure(upto):
                while loads and loads[0][0] <= upto:
                    loads.pop(0)[1]()

            nload = [0]
            for hd in range(NH):
                ob = nxt("O", 2)
                psO = P[1][:, 512:1024] if ob == 0 else P[2][:, 0:512]
                okey = ("P", 1, 1) if ob == 0 else ("P", 2, 0)
                if smp:
                    segs = [(q * 64, 64, [("old", 2 + q, blk, 8, q) for blk in range(past // 1024)] + [("diag_s", q)])
                            for q in range(2)]
                else:
                    nold = tl["t0"]
                    chunks = [("old", tl["seq"], blk, min(8, (nold - blk * 1024) // 128), None)
                              for blk in range((nold + 1023) // 1024)]
                    chunks += [("diag", dl) for dl in range(NT)]
                    segs = [(0, ntok, chunks)]
                last_item = None
                for (q0, nq, chunks) in segs:
                    nch = sum(c[3] if c[0] == "old" else 1 for c in chunks)
                    done = 0
                    for c in chunks:
                        if c[0] == "old":
                            _, slot, blk, nkc, q = c
                            lidx = nload[0]
                            nload[0] += 1
                            loads.append((lidx, mk_load(slot, hd, blk, nkc, lidx)))
                            for a_ in range(nkc):
                                kcg = blk * 8 + a_
                                if smp:
                                    nd = negDs[:, q * (past // 128) + kcg, hd:hd + 1]
                                    ndk = [("NDS", q)]
                                else:
                                    nd = negD[:, kcg, hd:hd + 1]
                                    ndk = [("ND", kcg)]
                                done += 1
                                pre = [(lambda l=lidx: ensure(l + 1))] if a_ == 0 else []
                                last_item = mk_old(hd, psO, okey, lidx, a_, q0, nq, nd, ndk, done == 1, done == nch, pre)
                                items.append(last_item)
                        elif c[0] == "diag":
                            done += 1
                            last_item = mk_diag(hd, psO, okey, c[1], nq, done == 1, done == nch)
                            items.append(last_item)
                        else:
                            done += 1
                            last_item = mk_diag_s(hd, psO, okey, c[1], done == 1, done == nch)
                            items.append(last_item)
                last_item["post"] = mk_norm(hd, psO, okey)
            for i in range(len(items) + LA):
                if i < len(items):
                    it = items[i]
                    for f in it["pre"]:
                        f()
                    it["qk"](it)
                    it["ex"](it)
                if i >= LA:
                    it = items[i - LA]
                    it["pv"](it)
                    if "post" in it:
                        it["post"]()

        def wo_proj(tl):
            NT = tl["NT"]
            slots = [ring_take("wo") for _ in range(2)]
            for j in range(NT):
                for dh in range(2):
                    n = nxt("bank", 6)
                    pn, half = n // 2, n % 2
                    s = slots[dh]

                    def fno(e, j=j, pn=pn, half=half, s=s):
                        ins = None
                        for pr in range(8):
                            ins = e.matmul(P[pn][:, half * 512:half * 512 + 512], lhsT=OT[:, pr, j * 128:(j + 1) * 128],
                                           rhs=ring[:, s, pr * 512:(pr + 1) * 512], start=(pr == 0), stop=(pr == 7))
                        return ins
                    sch.op("pe", fno, reads=[("ring", s)] + [A(8 + pr, j) for pr in range(8)], writes=[("P", pn, half)])
                    sch.op("dve", lambda e, j=j, pn=pn, half=half, dh=dh: e.tensor_tensor(
                        out=x[:, j, dh * 512:(dh + 1) * 512], in0=x[:, j, dh * 512:(dh + 1) * 512],
                        in1=P[pn][:, half * 512:half * 512 + 512], op=ALU.add),
                        reads=[("P", pn, half), ("x", j)], writes=[("x", j)])
            ring_release(2)

        def ingest():
            for q in range(2):
                slot = 2 + q
                sch.dma("sp", "d_lfc", lfc[:, :, :], cl[q].rearrange("(a p) c -> p a c", p=128), writes=["lfc"])
                sch.op("dve", lambda e: e.memset(Dprev[:, :], 0.0), writes=["Dprev"])
                for kg in range(past // 512):
                    for jj in range(4):
                        cum_pe(lfc[:, kg * 4 + jj, :], ["lfc"], False, jj)
                    for jj in range(4):
                        cum_dve(Dprev, "Dprev", negDs[:, q * (past // 128) + kg * 4 + jj, :], [("NDS", q)], None, None, jj)
                sch.op("dve", lambda e, q=q: e.tensor_copy(out=DprevS[q * 64:(q + 1) * 64, :], in_=Dprev[q * 64:(q + 1) * 64, :]),
                       reads=["Dprev"], writes=["DprevS"])
                for blk in range(past // 512):
                    for jj in range(4):
                        kc = blk * 4 + jj
                        rows = slice(kc * 128, (kc + 1) * 128)
                        fi = nxt("F", 3)
                        sch.dma("sp", "d_ingk%d" % jj, Fp[:, fi, :], ck[q, rows, :], writes=[("F", fi)])
                        sch.op("dve", lambda e, jj=jj, fi=fi: e.tensor_tensor(
                            out=qk_aug[:, jj, :, 0:64], in0=Fp[:, fi, :].rearrange("p (a d) -> p a d", d=64),
                            in1=gq_bc.unsqueeze(1).to_broadcast([128, NH, DH]), op=ALU.mult),
                            reads=[("F", fi)], writes=[("QK", jj)])
                        fi2 = nxt("F", 3)
                        sch.dma("sp", "d_ingv%d" % jj, Fp[:, fi2, :], cv[q, rows, :], writes=[("F", fi2)])

                        def fnv(e, jj=jj, fi2=fi2):
                            fv = Fp[:, fi2, :].rearrange("p (a b d) -> p a b d", b=2, d=64)
                            e.activation(out=Vaug[:, jj, :, 0:64], in_=fv[:, :, 0, :], func=AF.Copy)
                            return e.activation(out=Vaug[:, jj, :, 128:192], in_=fv[:, :, 1, :], func=AF.Copy)
                        sch.op("act", fnv, reads=[("F", fi2)], writes=[("VA", jj)])
                    sch.op("dve", lambda e: e.memset(qk_aug[:, :, :, 64], 1.0), writes=[("QK", j) for j in range(4)])
                    b0 = q * (past // 128) + blk * 4
                    nd_pieces(negDs[:, b0:b0 + 4, :], [("NDS", q)], 4)
                    qk_transposes(range(4), KT, "KT")
                    tk = store_kt(None, 512, slot, blk * 512, eng="sp")
                    tv = store_v_tile(None, slot, blk * 512, eng="sp")
            for kk in ing_keys:
                sch.res[kk]["w"] = tk
            sch.dma("pool", "d_spool", spool[0:15, :], spl[0, :, :], writes=["spool"])
            sch.dma("pool", "d_spool", spool[16:31, :], spl[1, :, :], writes=["spool"])

        eps_t = sb("eps_t", [128, 1], F32)
        one_t = sb("one_t", [128, 1], F32)
        sch.op("dve", lambda e: e.memset(eps_t[:, :], EPS), writes=["eps"])
        t1 = sch.op("dve", lambda e: e.memset(one_t[:, :], 1.0), writes=["one"])
        sch.wait_tok("act", t1)

        tiles = [{"kind": "s", "seq": 0, "t0": past, "NT": 1, "first": False, "last": True}]
        for seq in range(2):
            for ti in range(S // 512):
                tiles.append({"kind": "p", "seq": seq, "t0": ti * 512, "NT": 4, "first": ti == 0,
                              "last": ti == S // 512 - 1})
        for i in range(len(tiles) - 1):
            tiles[i]["next"] = tiles[i + 1]
        tiles[-1]["next"] = None

        load_x(tiles[0], 0)
        for j in range(1, 4):
            load_x(tiles[1], j)
        ingest()
        for _ in range(R_RING):
            ring_issue()
        for tl in tiles:
            if tl["kind"] == "p" and tl["first"]:
                sch.op("dve", lambda e: e.memset(Dprev[:, :], 0.0), writes=["Dprev"])
            pool_layer(tl)
            ffn(tl, 0)
            kvq(tl)
            attend(tl)
            wo_proj(tl)
            ffn(tl, 1)

        for sname, v in sch.cnt.items():
            if sname.startswith("d_"):
                sch.q["pool"].append(("wait", sname, v))

        sem_names = sorted(sch.cnt.keys())
        sems = {n: es.enter_context(nc.semaphore(n)) for n in sem_names}
        block = es.enter_context(nc.Block())

        def run(qname):
            def body(eng):
                for item in sch.q[qname]:
                    if item[0] == "wait":
                        eng.wait_ge(sems[item[1]], item[2])
                    else:
                        ins = item[1](eng)
                        ins.then_inc(sems[item[2]], item[3])
            return body

        block.tensor(run("pe"))
        block.scalar(run("act"))
        block.vector(run("dve"))
        block.gpsimd(run("pool"))
        block.sync(run("sp"))
    nc_sch_holder.append(sch)
    return nc


_CACHE = {}
nc_sch_holder = []


def _host_inputs(inp, c):
    f = lambda a: np.ascontiguousarray(np.asarray(a, dtype=np.float32))
    sl = slice(2 * c, 2 * c + 2)
    g_mix = np.asarray(inp["g_mix"], np.float32); g_ffn = np.asarray(inp["g_ffn"], np.float32)
    cols = lambda v: np.asarray(v, np.float32).reshape(8, 128).T
    gbc = np.concatenate([
        np.broadcast_to(g_mix[0][None, :], (128, D)),
        np.broadcast_to(np.asarray(inp["pool_scale"], np.float32)[0][None, :], (128, D)),
        np.broadcast_to(np.asarray(inp["g_qnorm"], np.float32)[0][None, :], (128, DH)),
        np.broadcast_to(np.asarray(inp["g_knorm"], np.float32)[None, :], (128, DH)),
        np.broadcast_to(np.asarray(inp["b_f"], np.float32)[None, :], (128, NH)),
        cols(g_ffn[0]), cols(inp["g_kv"]), cols(g_mix[1]), cols(g_ffn[1])], axis=1)
    B, S = inp["x_prompt"].shape[0], inp["x_prompt"].shape[1]
    past = inp["cache_k"].shape[1]
    return {
        "xp": f(inp["x_prompt"][sl]), "xs": f(inp["x_sample"][sl]), "spl": f(inp["state_pool"][0, sl]),
        "ck": f(np.asarray(inp["cache_k"][sl]).reshape(2, past, D)), "cv": f(np.asarray(inp["cache_v"][sl]).reshape(2, past, D)),
        "cl": f(inp["cache_logf"][sl]),
        "pool_w": f(np.asarray(inp["pool_w"])[0].reshape(D, 256)),
        "w_k": f(inp["w_k"]), "w_v": f(inp["w_v"]), "w_f": f(inp["w_f"]),
        "w_q": f(np.asarray(inp["w_q"])[0]), "w_o": f(np.asarray(inp["w_o"])[0]),
        "w_gate": f(inp["w_gate"]), "w_up": f(inp["w_up"]), "w_down": f(inp["w_down"]),
        "gbc": f(gbc),
    }


def kernel(**inputs):
    S = inputs["x_prompt"].shape[1]
    past = inputs["cache_k"].shape[1]
    ssl = inputs["x_sample"].shape[1]
    nb = inputs["x_prompt"].shape[0]
    ncores = nb // 2
    key = (S, past, ssl)
    if key not in _CACHE:
        _CACHE[key] = build_program(S, past, ssl)
    nc = _CACHE[key]
    cb, cf = _consts()
    in_maps = []
    for c in range(ncores):
        m = _host_inputs(inputs, c)
        m["cbf"] = cb
        m["cf32"] = cf
        in_maps.append(m)
    res = run_bass_kernel_spmd(nc, in_maps, core_ids=list(range(ncores)))
    rs = res.results
    cat = lambda n: np.concatenate([np.asarray(r[n], dtype=np.float32) for r in rs], axis=0)
    y_p = cat("y_p"); y_s = cat("y_s")
    ps_p = cat("ps_p")[None]; ps_s = cat("ps_s")[None]
    k_p = cat("k_p").reshape(nb, S, NH, DH); v_p = cat("v_p").reshape(nb, S, NH, DH); lf_p = cat("lf_p")
    k_s = cat("k_s").reshape(nb, ssl, NH, DH); v_s = cat("v_s").reshape(nb, ssl, NH, DH); lf_s = cat("lf_s")
    return (y_p, y_s, ps_p, ps_s, k_p, v_p, lf_p, k_s, v_s, lf_s)
```
